# Optimizing a Trainium2 kernel written in Bass

```python
import math
import jax, jax.numpy as jnp
from jax import lax
import numpy as np

D_MODEL = 1024
BATCH = 32
SEQ = 2048
DEPTH = 4

GRID_W = 64
CTX_LEN = 256
N_MIXERS = 2
HEAD_DIM = 64
N_HEADS = D_MODEL // HEAD_DIM
A_KV_HEADS = 4
A_GROUPS = N_HEADS // A_KV_HEADS
A_WINDOW = 128
A_BLOCK = 128
A_INNER = N_HEADS * HEAD_DIM
A_KV_DIM = A_KV_HEADS * HEAD_DIM
A_IN_DIM = 2 * A_INNER + 2 * A_KV_DIM
B_KH_MAX = 8
B_KW = 16
B_QW = 16
B_REG_W = 2 * B_KW
B_INNER = N_HEADS * HEAD_DIM
B_IN_DIM = 4 * B_INNER
ROPE_BASE = 10000.0
LN_EPS = 1e-5
NEG_INF = -1e30
DEEPNORM_ALPHA = (2.0 * DEPTH) ** 0.25
DEEPNORM_BETA = (8.0 * DEPTH) ** -0.25
N_LAYERS_A = (DEPTH + 1) // 2
N_LAYERS_B = DEPTH // 2

kernel_name = "hybrid_window_gqa_neighbourhood_diffusion_trunk"


def layer_norm(x, g, b):
    xf = x.astype(jnp.float32)
    mu = jnp.mean(xf, axis=-1, keepdims=True)
    var = jnp.mean(jnp.square(xf - mu), axis=-1, keepdims=True)
    y = (xf - mu) * lax.rsqrt(var + LN_EPS)
    return (y * g.astype(jnp.float32) + b.astype(jnp.float32)).astype(x.dtype)


def axial_rope_tables(n_tok):
    t = jnp.arange(n_tok)
    row = (t // GRID_W).astype(jnp.float32)
    col = (t % GRID_W).astype(jnp.float32)
    n_freq = HEAD_DIM // 4
    inv = ROPE_BASE ** (-jnp.arange(n_freq, dtype=jnp.float32) / n_freq)
    ang_r = row[:, None] * inv[None]
    ang_c = col[:, None] * inv[None]
    return (jnp.cos(ang_r), jnp.sin(ang_r), jnp.cos(ang_c), jnp.sin(ang_c))


def _rotate(x, cos, sin):
    a, b = jnp.split(x, 2, axis=-1)
    cos = cos[None, :, None, :].astype(x.dtype)
    sin = sin[None, :, None, :].astype(x.dtype)
    return jnp.concatenate([a * cos - b * sin, b * cos + a * sin], axis=-1)


def apply_axial_rope(x, rope):
    cos_r, sin_r, cos_c, sin_c = rope
    xr, xc = jnp.split(x, 2, axis=-1)
    return jnp.concatenate([_rotate(xr, cos_r, sin_r), _rotate(xc, cos_c, sin_c)], axis=-1)


def ctx_attention(q, k, v, sink):
    B, C = q.shape[0], q.shape[1]
    s = jnp.einsum('bqkgd,bckd->bkgqc', q, k).astype(jnp.float32) * (HEAD_DIM ** -0.5)
    if sink is not None:
        s_sink = jnp.broadcast_to(sink[None, :, :, None, None], s.shape[:-1] + (1,))
        p = jax.nn.softmax(jnp.concatenate([s, s_sink], axis=-1), axis=-1)[..., :C]
    else:
        p = jax.nn.softmax(s, axis=-1)
    o = jnp.einsum('bkgqc,bckd->bqkgd', p.astype(v.dtype), v)
    return o.reshape(B, C, -1)


def _split_a(p):
    return jnp.split(p, [A_INNER, A_INNER + A_KV_DIM, A_INNER + 2 * A_KV_DIM], axis=-1)


def mixer_a(u, uc, w_in, w_out, sink, rope, ctx_out):
    B, S, _ = u.shape
    C = uc.shape[1]
    scale = HEAD_DIM ** -0.5
    q, k, v, g = _split_a(u @ w_in)
    q = apply_axial_rope(q.reshape(B, S, N_HEADS, HEAD_DIM), rope).reshape(B, S, A_KV_HEADS, A_GROUPS, HEAD_DIM)
    k = apply_axial_rope(k.reshape(B, S, A_KV_HEADS, HEAD_DIM), rope)
    v = v.reshape(B, S, A_KV_HEADS, HEAD_DIM)
    qc, kc, vc, gc = _split_a(uc @ w_in)
    kc = kc.reshape(B, C, A_KV_HEADS, HEAD_DIM)
    vc = vc.reshape(B, C, A_KV_HEADS, HEAD_DIM)
    sink_l = sink.reshape(A_KV_HEADS, A_GROUPS).astype(jnp.float32)
    span = A_BLOCK + 2 * A_WINDOW
    pad = ((0, 0), (A_WINDOW, A_WINDOW), (0, 0), (0, 0))
    kp = jnp.pad(k, pad)
    vp = jnp.pad(v, pad)

    def block(bi):
        start = bi * A_BLOCK
        qb = lax.dynamic_slice_in_dim(q, start, A_BLOCK, axis=1)
        kb = lax.dynamic_slice_in_dim(kp, start, span, axis=1)
        vb = lax.dynamic_slice_in_dim(vp, start, span, axis=1)
        qi = start + jnp.arange(A_BLOCK)
        kj = start - A_WINDOW + jnp.arange(span)
        mask = (jnp.abs(qi[:, None] - kj[None, :]) <= A_WINDOW) & (kj >= 0)[None, :] & (kj < S)[None, :]
        s_lat = jnp.einsum('bqkgd,bjkd->bkgqj', qb, kb).astype(jnp.float32) * scale
        s_lat = jnp.where(mask, s_lat, NEG_INF)
        s_ctx = jnp.einsum('bqkgd,bckd->bkgqc', qb, kc).astype(jnp.float32) * scale
        s_sink = jnp.broadcast_to(sink_l[None, :, :, None, None], s_lat.shape[:-1] + (1,))
        p = jax.nn.softmax(jnp.concatenate([s_lat, s_ctx, s_sink], axis=-1), axis=-1)
        p_lat = p[..., :span].astype(v.dtype)
        p_ctx = p[..., span:span + C].astype(v.dtype)
        o = jnp.einsum('bkgqj,bjkd->bqkgd', p_lat, vb) + jnp.einsum('bkgqc,bckd->bqkgd', p_ctx, vc)
        return o.reshape(B, A_BLOCK, A_INNER)

    o = lax.map(block, jnp.arange(S // A_BLOCK))
    o = jnp.transpose(o, (1, 0, 2, 3)).reshape(B, S, A_INNER)
    y = (o * jax.nn.silu(g)) @ w_out
    if not ctx_out:
        return y, None
    oc = ctx_attention(qc.reshape(B, C, A_KV_HEADS, A_GROUPS, HEAD_DIM), kc, vc, sink_l)
    yc = (oc * jax.nn.silu(gc)) @ w_out
    return y, yc


def mixer_b(u, uc, w_in, w_out, rel_bias, ctx_out):
    B, S, _ = u.shape
    C = uc.shape[1]
    rows = S // GRID_W
    kh = min(B_KH_MAX, rows)
    scale = HEAD_DIM ** -0.5
    q, k, v, g = jnp.split(u @ w_in, 4, axis=-1)
    q = q.reshape(B, rows, GRID_W, N_HEADS, HEAD_DIM)
    k = k.reshape(B, rows, GRID_W, N_HEADS, HEAD_DIM)
    v = v.reshape(B, rows, GRID_W, N_HEADS, HEAD_DIM)
    qc, kc, vc, gc = jnp.split(uc @ w_in, 4, axis=-1)
    kc = kc.reshape(B, C, N_HEADS, HEAD_DIM)
    vc = vc.reshape(B, C, N_HEADS, HEAD_DIM)

    n_cb = GRID_W // B_QW
    c0 = np.arange(n_cb) * B_QW
    cs_blk = np.clip(c0 - B_KW // 2, 0, GRID_W - B_REG_W)
    key_col = cs_blk[:, None] + np.arange(B_REG_W)
    q_col = c0[:, None] + np.arange(B_QW)
    cs_q = np.clip(q_col - B_KW // 2, 0, GRID_W - B_KW)
    col_mask = (key_col[:, None, :] >= cs_q[:, :, None]) & (key_col[:, None, :] < cs_q[:, :, None] + B_KW)
    col_idx = np.clip(key_col[:, None, :] - q_col[:, :, None] + (B_KW - 1), 0, 2 * B_KW - 2)
    key_col = jnp.asarray(key_col, dtype=jnp.int32)
    col_idx = jnp.asarray(col_idx, dtype=jnp.int32)
    col_mask = jnp.asarray(col_mask)[:, :, None, :]
    bias_tab = rel_bias.astype(jnp.float32)

    def row_block(r):
        rs = jnp.clip(r - kh // 2, 0, rows - kh)
        q_r = lax.dynamic_index_in_dim(q, r, axis=1, keepdims=False).reshape(B, n_cb, B_QW, N_HEADS, HEAD_DIM)
        k_rows = lax.dynamic_slice_in_dim(k, rs, kh, axis=1)
        v_rows = lax.dynamic_slice_in_dim(v, rs, kh, axis=1)
        k_reg = k_rows[:, :, key_col]
        v_reg = v_rows[:, :, key_col]
        s = jnp.einsum('bnqhd,brnjhd->bhnqrj', q_r, k_reg).astype(jnp.float32) * scale
        row_idx = rs + jnp.arange(kh) - r + (B_KH_MAX - 1)
        bias = bias_tab[:, row_idx[:, None, None, None], col_idx[None]]
        s = s + jnp.transpose(bias, (0, 2, 3, 1, 4))[None]
        s = jnp.where(col_mask, s, NEG_INF).reshape(B, N_HEADS, n_cb, B_QW, kh * B_REG_W)
        s_ctx = jnp.einsum('bnqhd,bchd->bhnqc', q_r, kc).astype(jnp.float32) * scale
        p = jax.nn.softmax(jnp.concatenate([s, s_ctx], axis=-1), axis=-1)
        p_lat = p[..., :kh * B_REG_W].reshape(B, N_HEADS, n_cb, B_QW, kh, B_REG_W).astype(v.dtype)
        p_ctx = p[..., kh * B_REG_W:].astype(v.dtype)
        o = jnp.einsum('bhnqrj,brnjhd->bnqhd', p_lat, v_reg) + jnp.einsum('bhnqc,bchd->bnqhd', p_ctx, vc)
        return o.reshape(B, GRID_W, B_INNER)

    o = lax.map(row_block, jnp.arange(rows))
    o = jnp.transpose(o, (1, 0, 2, 3)).reshape(B, S, B_INNER)
    y = (o * jax.nn.silu(g)) @ w_out
    if not ctx_out:
        return y, None
    oc = ctx_attention(qc.reshape(B, C, N_HEADS, 1, HEAD_DIM), kc[:, :, :, :], vc, None)
    yc = (oc * jax.nn.silu(gc)) @ w_out
    return y, yc


def setup_inputs(seed: int = 0) -> dict:
    key = jax.random.key(seed)
    ks = jax.random.split(key, 14)
    f32 = jnp.float32
    d = D_MODEL
    return {
        "x": jax.random.normal(ks[0], (BATCH, SEQ, d), f32),
        "c": jax.random.normal(ks[1], (BATCH, d), f32),
        "ctx": jax.random.normal(ks[2], (BATCH, CTX_LEN, d), f32),
        "c_ctx": jax.random.normal(ks[3], (d,), f32),
        "w_ada": jax.random.normal(ks[4], (DEPTH, d, 3 * d), f32) * d ** -0.5,
        "b_ada": 0.01 * jax.random.normal(ks[5], (DEPTH, 3 * d), f32),
        "ln_g": 1.0 + 0.05 * jax.random.normal(ks[6], (DEPTH, d), f32),
        "ln_b": 0.02 * jax.random.normal(ks[7], (DEPTH, d), f32),
        "a_w_in": jax.random.normal(ks[8], (N_LAYERS_A, d, A_IN_DIM), f32) * d ** -0.5,
        "a_w_out": jax.random.normal(ks[9], (N_LAYERS_A, A_INNER, d), f32) * (A_INNER ** -0.5 * DEEPNORM_BETA),
        "a_sink": 0.5 * jax.random.normal(ks[10], (N_LAYERS_A, N_HEADS), f32),
        "b_w_in": jax.random.normal(ks[11], (N_LAYERS_B, d, B_IN_DIM), f32) * d ** -0.5,
        "b_w_out": jax.random.normal(ks[12], (N_LAYERS_B, B_INNER, d), f32) * (B_INNER ** -0.5 * DEEPNORM_BETA),
        "b_rel_bias": 0.5 * jax.random.normal(ks[13], (N_LAYERS_B, N_HEADS, 2 * B_KH_MAX - 1, 2 * B_KW - 1), f32),
    }


def reference(x, c, ctx, c_ctx, w_ada, b_ada, ln_g, ln_b, a_w_in, a_w_out, a_sink, b_w_in, b_w_out, b_rel_bias):
    S = x.shape[1]
    rope = axial_rope_tables(S)
    silu_c = jax.nn.silu(c)
    silu_cc = jax.nn.silu(c_ctx)
    h, hc = x, ctx
    for i in range(DEPTH):
        shift, scale, gate = jnp.split(silu_c @ w_ada[i] + b_ada[i], 3, axis=-1)
        shift_c, scale_c, gate_c = jnp.split(silu_cc @ w_ada[i] + b_ada[i], 3, axis=-1)
        ctx_out = i < DEPTH - 1
        u = h * (1.0 + scale[:, None, :]) + shift[:, None, :]
        uc = hc * (1.0 + scale_c) + shift_c
        j = i // N_MIXERS
        if i % N_MIXERS == 0:
            y, yc = mixer_a(u, uc, a_w_in[j], a_w_out[j], a_sink[j], rope, ctx_out)
        else:
            y, yc = mixer_b(u, uc, b_w_in[j], b_w_out[j], b_rel_bias[j], ctx_out)
        h = layer_norm(DEEPNORM_ALPHA * h + gate[:, None, :] * y, ln_g[i], ln_b[i])
        if ctx_out:
            hc = layer_norm(DEEPNORM_ALPHA * hc + gate_c * yc, ln_g[i], ln_b[i])
    return h
```

```python
import numpy as np
import concourse.bass as bass
import concourse.mybir as mybir
from concourse.bass_utils import run_bass_kernel_spmd

F32 = mybir.dt.float32
BF16 = mybir.dt.bfloat16
AF = mybir.ActivationFunctionType
ALU = mybir.AluOpType

D = 1024
S = 2048
C = 256
NT = 18
DEPTH = 4
ALPHA = (2.0 * DEPTH) ** 0.25
EPS = 1e-5
NCORES = 8


class P:
    def __init__(self, nc):
        self.nc = nc
        self.eng = {"pe": nc.tensor, "act": nc.scalar, "dve": nc.vector, "pool": nc.gpsimd, "sp": nc.sync}
        self.sems = {}
        self.cnt = {}
        for k in ["pe", "act", "dve", "pool"]:
            self.sems[k] = nc.alloc_semaphore("e_" + k)
            self.cnt[k] = 0
        self.waited = {k: {} for k in self.eng}
        self.lastw = {}
        self.readers = {}
        self.tag = ""
        self.pe_tags = []

    def _need(self, e, toks):
        for (s, v) in toks:
            if v <= 0:
                continue
            if s == "pe" and e == "pe":
                continue
            if self.waited[e].get(s, 0) >= v:
                continue
            self.eng[e].wait_ge(self.sems[s], v)
            self.waited[e][s] = v

    def emit(self, e, fn, r=(), w=(), dma=None):
        toks = []
        for res in r:
            if res in self.lastw:
                toks.append(self.lastw[res])
        for res in w:
            if res in self.lastw:
                toks.append(self.lastw[res])
            for s, v in self.readers.get(res, {}).items():
                toks.append((s, v))
        self._need(e, toks)
        ins = fn(self.eng[e])
        if e == "pe":
            self.pe_tags.append(self.tag)
        if dma is None:
            self.cnt[e] += 1
            ins.then_inc(self.sems[e], 1)
            tok = (e, self.cnt[e])
        else:
            if dma not in self.sems:
                self.sems[dma] = self.nc.alloc_semaphore("d_" + dma)
                self.cnt[dma] = 0
            self.cnt[dma] += 16
            ins.then_inc(self.sems[dma], 16)
            tok = (dma, self.cnt[dma])
        for res in w:
            self.lastw[res] = tok
            self.readers[res] = {}
        for res in r:
            d = self.readers.setdefault(res, {})
            d[tok[0]] = max(d.get(tok[0], 0), tok[1])
        return tok

    def barrier(self):
        toks = [(s, v) for s, v in self.cnt.items()]
        for e in self.eng:
            self._need(e, toks)


def _layer_cfg(i):
    if i % 2 == 0:
        return dict(kind="A", NG=4, nq=4, nkv=1, XW=320, YW=320, GW=640)
    return dict(kind="B", NG=8, nq=2, nkv=2, XW=256, YW=256, GW=512)


def build_nc(NB=4, NL=DEPTH):
    nc = bass.Bass("TRN2", target_bir_lowering=False)
    dt = nc.dram_tensor
    x = dt("x", [NB, S, D], F32, kind="ExternalInput").ap()
    ctx = dt("ctx", [NB, C, D], F32, kind="ExternalInput").ap()
    out = dt("out", [NB, S, D], F32, kind="ExternalOutput").ap()
    cT = dt("cT", [128, 8, 5], F32, kind="ExternalInput").ap()
    w_ada = dt("w_ada", [DEPTH, D, 3 * D], F32, kind="ExternalInput").ap()
    badaT = dt("badaT", [128, DEPTH, 16], F32, kind="ExternalInput").ap()
    bgate = dt("bgate", [DEPTH, D], F32, kind="ExternalInput").ap()
    ln_g = dt("ln_g", [DEPTH, D], F32, kind="ExternalInput").ap()
    ln_b = dt("ln_b", [DEPTH, D], F32, kind="ExternalInput").ap()
    wA = dt("wA", [2, 4, 128, 8 * 640], F32, kind="ExternalInput").ap()
    wB = dt("wB", [2, 8, 128, 8 * 512], F32, kind="ExternalInput").ap()
    woA = dt("woA", [2, 128, 8 * D], F32, kind="ExternalInput").ap()
    woB = dt("woB", [2, 128, 8 * D], F32, kind="ExternalInput").ap()
    sinkA = dt("sinkA", [2, 16], F32, kind="ExternalInput").ap()
    biasG = dt("biasG", [2, 16, 128, 21 * 128], F32, kind="ExternalInput").ap()
    maskB_d = dt("maskB", [128, 21 * 128], F32, kind="ExternalInput").ap()
    maskA_d = dt("maskA", [128, 256], F32, kind="ExternalInput").ap()
    cosx_d = dt("cosx", [128, 16 * 64], F32, kind="ExternalInput").ap()
    sinx_d = dt("sinx", [128, 16 * 64], F32, kind="ExternalInput").ap()
    ident_d = dt("ident", [128, 128], F32, kind="ExternalInput").ap()
    hscr = dt("hscr", [2, S + C, D], F32).ap()
    gscr = dt("gscr", [DEPTH, 5, D], F32).ap()

    sb = nc.alloc_sbuf_tensor
    uT = sb("uT", [128, 8, NT * 128], BF16)
    OGT = sb("OGT", [128, 8, NT * 128], BF16)
    wg = [sb("wg%d" % i, [128, 8, 640], BF16) for i in range(2)]
    wo = sb("wo", [128, 8, D], BF16)
    qk_tm = [sb("qktm%d" % i, [128, 320], BF16) for i in range(2)]
    Pb = [sb("Pb%d" % i, [128, 896], BF16) for i in range(2)]
    EB = sb("EB", [128, 2, 21 * 128], BF16)
    ebraw = sb("ebraw", [128, 11 * 128], F32)
    maskB = sb("maskBs", [128, 21 * 128], BF16)
    maskA = sb("maskAs", [128, 256], BF16)
    cosx = sb("cosxs", [128, 16, 64], F32)
    sinx = sb("sinxs", [128, 16, 64], F32)
    idf = sb("idf", [128, 128], F32)
    idb = sb("idb", [128, 128], BF16)
    modT = sb("modT", [128, DEPTH, 16, 5], F32)
    bada = sb("badas", [128, DEPTH, 16], F32)
    cTs = sb("cTs", [128, 8, 5], F32)
    s2 = sb("s2", [128, 8, 5], F32)
    esink = sb("esink", [128, 16], F32)
    OGb = [sb("OGb%d" % i, [128, 256], BF16) for i in range(2)]
    th2 = [sb("th%d" % i, [128, 256], BF16) for i in range(2)]
    tmpA2 = sb("tmpA2", [128, D], F32)
    tmpB2 = sb("tmpB2", [128, D], F32)
    rt1 = sb("rt1", [128, 320], F32)
    rt2 = sb("rt2", [128, 320], F32)
    sm = [sb("sm%d" % i, [128, 8], F32) for i in range(2)]
    lst = [sb("lst%d" % i, [128, 16], F32) for i in range(2)]
    mhalf = sb("mhalf", [128, 1], F32)
    onesb = sb("onesb", [1, 128], BF16)
    srow = sb("srow", [1, 2, 16, 65], BF16)
    sinkst = sb("sinkst", [1, 32], F32)
    roff = (nc.sbuf_base + 31) // 32 * 32
    Rbuf = sb("Rbuf", [128, 41600 + 64], mybir.dt.uint8)
    assert nc.sbuf_base >= roff + 41600, (nc.sbuf_base, roff)

    def at(name, shape, dtype, off):
        return nc.alloc_sbuf_tensor_at(name, shape, dtype, offset=roff + off)

    QT = at("QT", [64, 4, NT * 128], BF16, 0)
    KT = at("KT", [64, 2, NT * 128], BF16, 18432)
    V = at("V", [128, NT, 2, 65], BF16, 27648)
    SG = at("SG", [128, NT, 256], BF16, 32384)
    hin = [at("hin%d" % i, [128, D], F32, 4096 * i) for i in range(2)]
    hout = [at("hout%d" % i, [128, D], F32, 8192 + 4096 * i) for i in range(2)]
    tmpA = at("tmpA", [128, D], F32, 16384)
    tmpB = at("tmpB", [128, D], F32, 20480)
    lng = at("lng", [128, D], F32, 24576)
    lnb = at("lnb", [128, D], F32, 28672)
    gate = at("gate", [128, D], F32, 32768)
    gatec = at("gatec", [128, D], F32, 36864)
    wa = at("wa", [128, 8, 512], F32, 0)
    gt = at("gt", [5, 512], F32, 16384)
    bg5 = at("bg5", [5, D], F32, 20480)

    PSall = nc.alloc_psum_tensor("psall", [128, 8, 512], F32)
    PS = [PSall[:, i, :] for i in range(8)]

    def psb(i, dtype):
        return PS[i][:].bitcast(dtype) if dtype != F32 else PS[i][:]

    p = P(nc)
    E = p.emit

    E("sp", lambda e: e.dma_start(out=idf[:], in_=ident_d[:, :]), w=["idf"], dma="c0")
    E("sp", lambda e: e.dma_start(out=cosx[:].rearrange("p t d -> p (t d)"), in_=cosx_d[:, :]), w=["cosx"], dma="c1")
    E("sp", lambda e: e.dma_start(out=sinx[:].rearrange("p t d -> p (t d)"), in_=sinx_d[:, :]), w=["sinx"], dma="c2")
    E("pool", lambda e: e.dma_start(out=maskB[:], in_=maskB_d[:, :], max_dma_last_dim=4096), w=["maskB"], dma="c3")
    E("pool", lambda e: e.dma_start(out=maskA[:], in_=maskA_d[:, :]), w=["maskA"], dma="c4")
    E("sp", lambda e: e.dma_start(out=cTs[:].rearrange("p k j -> p (k j)"), in_=cT.rearrange("p k j -> p (k j)")), w=["cTs"], dma="c5")
    E("sp", lambda e: e.dma_start(out=bada[:].rearrange("p l c -> p (l c)"), in_=badaT.rearrange("p l c -> p (l c)")), w=["bada"], dma="c6")
    E("dve", lambda e: e.tensor_copy(out=idb[:], in_=idf[:]), r=["idf"], w=["idb"])
    E("dve", lambda e: e.memset(mhalf[:], -0.5), w=["mhalf"])
    E("dve", lambda e: e.memset(onesb[:], 1.0), w=["onesb"])
    E("dve", lambda e: e.memset(srow[:], 0.0), w=["srow"])
    E("sp", lambda e: e.dma_start(out=sinkst[:], in_=sinkA.rearrange("l h -> (l h)").partition_broadcast(1)), w=["sinkst"], dma="esink")
    E("act", lambda e: e.activation(out=sinkst[:], in_=sinkst[:], func=AF.Exp), w=["sinkst"])
    for l_ in range(2):
        E("dve", lambda e: e.tensor_scalar(out=srow[0:1, l_, :, 64], in0=sinkst[0:1, l_ * 16:(l_ + 1) * 16], scalar1=2.0, scalar2=None, op0=ALU.mult),
          r=["sinkst"], w=["srow"])
    E("dve", lambda e: e.tensor_scalar(out=bada[:, :, 8:16], in0=bada[:, :, 8:16], scalar1=1.0, scalar2=None, op0=ALU.add), r=["bada"], w=["bada"])
    E("act", lambda e: e.activation(out=s2[:], in_=cTs[:], func=AF.Tanh, scale=0.5), r=["cTs"], w=["s2"])
    E("dve", lambda e: e.scalar_tensor_tensor(out=s2[:], in0=s2[:], scalar=1.0, in1=cTs[:], op0=ALU.add, op1=ALU.mult), r=["s2", "cTs"], w=["s2"])

    for l in range(NL):
        E("sp", lambda e: e.dma_start(out=bg5[:], in_=bgate[l].partition_broadcast(5)), w=["bg5"], dma="bg5")
        for blk in range(6):
            src = w_ada[l, :, blk * 512:(blk + 1) * 512].rearrange("(k p) n -> p k n", p=128)
            E("sp", lambda e: e.dma_start(out=wa[:], in_=src), w=["wa"], dma="wa")
            if blk < 4:
                for cc in range(4):
                    nch = blk * 4 + cc
                    for k in range(8):
                        E("pe", lambda e: e.matmul(PS[7][:, cc * 8:cc * 8 + 5], wa[:, k, cc * 128:(cc + 1) * 128], s2[:, k, :],
                                                   start=(k == 0), stop=(k == 7)), r=["wa", "s2"], w=["ps7"])
                    E("act", lambda e: e.activation(out=modT[:, l, nch, :], in_=PS[7][:, cc * 8:cc * 8 + 5], func=AF.Identity,
                                                    scale=0.5, bias=bada[:, l, nch:nch + 1]), r=["bada"], w=["ps7", "modT"])
            else:
                for k in range(8):
                    E("pe", lambda e: e.matmul(PS[7][0:5, :], s2[:, k, :], wa[:, k, :], start=(k == 0), stop=(k == 7)),
                      r=["wa", "s2"], w=["ps7"])
                c0 = (blk - 4) * 512
                E("act", lambda e: e.activation(out=gt[:], in_=PS[7][0:5, :], func=AF.Copy, scale=0.5), w=["ps7", "gt"])
                E("dve", lambda e: e.tensor_tensor(out=gt[:], in0=gt[:], in1=bg5[:, c0:c0 + 512], op=ALU.add), r=["bg5"], w=["gt"])
                E("sp", lambda e: e.dma_start(out=gscr[l, :, c0:c0 + 512], in_=gt[:]), r=["gt"], w=["gscr"], dma="gs")
    p.barrier()

    def tok_src(b, i, t):
        if i == 0:
            if t < 16:
                return x[b, t * 128:(t + 1) * 128, :]
            return ctx[b, (t - 16) * 128:(t - 15) * 128, :]
        return hscr[(i - 1) % 2, t * 128:(t + 1) * 128, :]

    def hres(i, t):
        return "h%d_%d" % (i, t)

    def make_uT(b, i, t, src_tile, src_res):
        j = b if t < 16 else 4
        pb = 4 + 2 * (t % 2)
        for c in range(8):
            bank = pb + c // 4
            E("pe", lambda e: e.transpose(PS[bank][:, (c % 4) * 128:(c % 4 + 1) * 128], src_tile[:, c * 128:(c + 1) * 128], idf[:]),
              r=[src_res, "idf"], w=["ps%d" % bank])
        for c in range(8):
            bank = pb + c // 4
            E("act", lambda e: e.activation(out=uT[:, c, t * 128:(t + 1) * 128], in_=PS[bank][:, (c % 4) * 128:(c % 4 + 1) * 128],
                                            func=AF.Identity, scale=modT[:, i, 8 + c, j:j + 1], bias=modT[:, i, c, j:j + 1]),
              r=["modT"], w=["ps%d" % bank, "uT%d" % t])

    E("dve", lambda e: e.memset(V[:], 2.0), w=["V%d" % t for t in range(NT)])

    for b in range(NB):
        p.barrier()
        p.tag = "init"
        for t in range(NT):
            hb = hin[t % 2]
            E("sp", lambda e: e.dma_start(out=hb[:], in_=tok_src(b, 0, t)), w=["hin%d" % (t % 2)], dma="hin%d" % (t % 2))
            make_uT(b, 0, t, hb, "hin%d" % (t % 2))
        p.barrier()
        E("dve", lambda e: e.memset(V[:], 2.0), w=["V%d" % t for t in range(NT)])

        for i in range(NL):
            cfg = _layer_cfg(i)
            kind, NG, nq, nkv, XW, YW, GW = (cfg[k] for k in ["kind", "NG", "nq", "nkv", "XW", "YW", "GW"])
            li = i // 2
            ctx_out = i < NL - 1
            last = i == NL - 1
            wsrc = wA if kind == "A" else wB
            wosrc = woA if kind == "A" else woB
            NQT = 18 if ctx_out else 16

            def load_wg(gi, ii=i):
                c_ = _layer_cfg(ii)
                src_ = (wA if c_["kind"] == "A" else wB)[ii // 2, gi]
                dst = wg[gi % 2][:, :, 0:c_["GW"]]
                E("pool", lambda e: e.dma_start(out=dst, in_=src_.rearrange("p (k n) -> p k n", k=8), max_dma_last_dim=4096),
                  w=["wg%d" % (gi % 2)], dma="wg%d" % (gi % 2))

            if b == 0 and i == 0:
                load_wg(0)
            for gi in range(NG):
                if gi + 1 < NG:
                    load_wg(gi + 1)
                    if gi == 0:
                        E("pool", lambda e: e.dma_start(out=wo[:].rearrange("p c n -> p (c n)"), in_=wosrc[li], max_dma_last_dim=4096), w=["wo"], dma="wo")
                elif not (b == NB - 1 and i == NL - 1):
                    load_wg(0, (i + 1) % NL)
                wgt = wg[gi % 2]
                wres = "wg%d" % (gi % 2)
                eb_jobs = []
                if kind == "B":
                    for hh in range(2):
                        for (c0, c1) in [(0, 11 * 128), (11 * 128, 21 * 128)]:
                            eb_jobs.append((hh, c0, c1))

                def EBDMA(j):
                    hh, c0, c1 = eb_jobs[j]
                    E("sp", lambda e: e.dma_start(out=ebraw[:, 0:c1 - c0], in_=biasG[li, gi * 2 + hh, :, c0:c1]), w=["ebraw"], dma="ebraw")

                def EBPROC(j):
                    hh, c0, c1 = eb_jobs[j]
                    E("dve", lambda e: e.tensor_tensor(out=ebraw[:, 0:c1 - c0], in0=ebraw[:, 0:c1 - c0], in1=maskB[:, c0:c1], op=ALU.add),
                      r=["maskB"], w=["ebraw"])
                    E("act", lambda e: e.activation(out=EB[:, hh, c0:c1], in_=ebraw[:, 0:c1 - c0], func=AF.Exp), r=["ebraw"], w=["EB"])

                p.tag = "proj%s" % kind
                nh = XW // 64
                gw = nq * 64

                def MM(t):
                    bx = 2 * (t % 2)
                    by = bx + 1
                    for k in range(8):
                        E("pe", lambda e: e.matmul(PS[bx][:, 0:XW], uT[:, k, t * 128:(t + 1) * 128], wgt[:, k, 0:XW], start=(k == 0), stop=(k == 7)),
                          r=["uT%d" % t, wres], w=["ps%d" % bx])
                    for k in range(8):
                        E("pe", lambda e: e.matmul(PS[by][:, 0:YW], uT[:, k, t * 128:(t + 1) * 128], wgt[:, k, XW:XW + YW], start=(k == 0), stop=(k == 7)),
                          r=["uT%d" % t, wres], w=["ps%d" % by])

                def EV(t):
                    bx = 2 * (t % 2)
                    by = bx + 1
                    qt = qk_tm[t % 2]
                    qres = "qktm%d" % (t % 2)
                    if kind == "A" and t < 16:
                        NH = 5
                        xin = PS[bx][:, 0:320].rearrange("p (h d) -> p h d", h=NH)
                        cb = cosx[:, t, :]
                        cb = bass.AP(cb.tensor, cb.offset, [list(cb.ap[0]), [0, NH], [1, 64]])
                        E("dve", lambda e: e.tensor_tensor(out=rt1[:].rearrange("p (h d) -> p h d", h=NH), in0=xin, in1=cb, op=ALU.mult),
                          r=["cosx"], w=["ps%d" % bx, "rt1"])
                        sbp = sinx[:, t, :]
                        for a_ in range(2):
                            sa = bass.AP(sbp.tensor, sbp.offset + a_ * 16, [list(sbp.ap[0]), [0, NH], [32, 2], [1, 16]])
                            xa = PS[bx][:, 0:320].rearrange("p (h r a f) -> p h r a f", h=NH, r=2, a=2, f=16)[:, :, :, 1 - a_, :]
                            ra = rt2[:].rearrange("p (h r a f) -> p h r a f", h=NH, r=2, a=2, f=16)[:, :, :, a_, :]
                            E("dve", lambda e: e.tensor_tensor(out=ra, in0=xa, in1=sa, op=ALU.mult), r=["sinx"], w=["ps%d" % bx, "rt2"])
                        E("pool", lambda e: e.tensor_tensor(out=qt[:, 0:320], in0=rt1[:], in1=rt2[:], op=ALU.add), r=["rt1", "rt2"], w=[qres])
                    else:
                        E("dve", lambda e: e.tensor_copy(out=qt[:, 0:XW], in_=PS[bx][:, 0:XW]), w=["ps%d" % bx, qres])
                    E("act", lambda e: e.activation(out=V[:, t, 0:nkv, 0:64], in_=PS[by][:, 0:nkv * 64].rearrange("p (h d) -> p h d", h=nkv), func=AF.Copy),
                      w=["ps%d" % by, "V%d" % t])
                    th = th2[t % 2]
                    E("act", lambda e: e.activation(out=th[:, 0:gw], in_=PS[by][:, nkv * 64:nkv * 64 + gw], func=AF.Tanh, scale=0.5),
                      w=["ps%d" % by, "th%d" % (t % 2)])
                    E("dve", lambda e: e.scalar_tensor_tensor(out=SG[:, t, 0:gw], in0=th[:, 0:gw], scalar=1.0, in1=PS[by][:, nkv * 64:nkv * 64 + gw],
                                                              op0=ALU.add, op1=ALU.mult), r=["th%d" % (t % 2)], w=["ps%d" % by, "SG%d" % t])

                def TR(t):
                    qt = qk_tm[t % 2]
                    qres = "qktm%d" % (t % 2)
                    tb = 4 + (t % 2)
                    ptr = PS[tb][:].bitcast(BF16)
                    for hh in range(nh):
                        E("pe", lambda e: e.transpose(ptr[0:64, hh * 128:(hh + 1) * 128], qt[:, hh * 64:(hh + 1) * 64], idb[:]),
                          r=[qres, "idb"], w=["ps%d" % tb])

                def TRE(t):
                    tb = 4 + (t % 2)
                    ptr = PS[tb][:].bitcast(BF16)
                    E("act", lambda e: e.activation(out=QT[:, 0:nq, t * 128:(t + 1) * 128],
                                                    in_=ptr[0:64, 0:nq * 128].rearrange("p (h n) -> p h n", h=nq), func=AF.Copy),
                      w=["ps%d" % tb, "QT%d" % t])
                    E("act", lambda e: e.activation(out=KT[:, 0:nkv, t * 128:(t + 1) * 128],
                                                    in_=ptr[0:64, nq * 128:(nq + nkv) * 128].rearrange("p (h n) -> p h n", h=nkv), func=AF.Copy),
                      w=["ps%d" % tb, "KT%d" % t])

                if eb_jobs:
                    EBDMA(0)
                MM(0)
                for t in range(NT):
                    if t + 1 < NT:
                        MM(t + 1)
                    EV(t)
                    if t > 0:
                        TRE(t - 1)
                    TR(t)
                    if eb_jobs and t in (3, 7, 11, 15):
                        j = (t - 3) // 4
                        EBPROC(j)
                        if j + 1 < len(eb_jobs):
                            EBDMA(j + 1)
                TRE(NT - 1)

                p.tag = "attn%s" % kind
                its = []
                for m in range(NQT):
                    eb0 = None
                    if m >= 16:
                        kts = [16, 17]
                        nmask = 0
                        mwhich = None
                    elif kind == "A":
                        kts = []
                        mwhich = []
                        if m > 0:
                            kts.append(m - 1)
                            mwhich.append(0)
                        if m < 15:
                            kts.append(m + 1)
                            mwhich.append(1)
                        nmask = len(kts)
                        kts += [m, 16, 17]
                    else:
                        mwhich = None
                        if m == 0:
                            lat, eb0 = [0, 1, 2, 3], 0
                        elif m == 1:
                            lat, eb0 = [0, 1, 2, 3], 4
                        elif m == 14:
                            lat, eb0 = [12, 13, 14, 15], 13
                        elif m == 15:
                            lat, eb0 = [12, 13, 14, 15], 17
                        else:
                            lat, eb0 = [m - 2, m - 1, m, m + 1, m + 2], 8
                        nmask = len(lat)
                        kts = lat + [16, 17]
                    for hq in range(nq):
                        its.append(dict(m=m, hq=hq, kts=kts, nmask=nmask, eb0=eb0, mwhich=mwhich))

                def SA(k):
                    d_ = its[k]
                    m, hq, kts = d_["m"], d_["hq"], d_["kts"]
                    hk = 0 if kind == "A" else hq
                    sbk = 2 * (k % 2)
                    for idx, kt in enumerate(kts):
                        bank = sbk + idx // 4
                        E("pe", lambda e: e.matmul(PS[bank][:, (idx % 4) * 128:(idx % 4 + 1) * 128], KT[:, hk, kt * 128:(kt + 1) * 128],
                                                   QT[:, hq, m * 128:(m + 1) * 128], start=True, stop=True),
                          r=["KT%d" % kt, "QT%d" % m], w=["ps%d" % bank])

                def SB(k):
                    d_ = its[k]
                    hq, kts, nmask, eb0, mwhich = d_["hq"], d_["kts"], d_["nmask"], d_["eb0"], d_["mwhich"]
                    n = len(kts)
                    sbk = 2 * (k % 2)
                    pb_ = Pb[k % 2]
                    pres = "Pb%d" % (k % 2)
                    sview = PSall[:, sbk:sbk + 2, :].rearrange("p b n -> p (b n)")
                    E("act", lambda e: e.activation(out=pb_[:, 0:n * 128], in_=sview[:, 0:n * 128], func=AF.Exp, scale=0.125),
                      w=["ps%d" % sbk, pres] + (["ps%d" % (sbk + 1)] if n > 4 else []))
                    if nmask > 0:
                        if kind == "A":
                            msk = maskA[:, 0:256] if nmask == 2 else maskA[:, mwhich[0] * 128:(mwhich[0] + 1) * 128]
                            E("dve", lambda e: e.tensor_tensor(out=pb_[:, 0:nmask * 128], in0=pb_[:, 0:nmask * 128], in1=msk, op=ALU.mult),
                              r=["maskA"], w=[pres])
                        else:
                            E("dve", lambda e: e.tensor_tensor(out=pb_[:, 0:nmask * 128], in0=pb_[:, 0:nmask * 128],
                                                               in1=EB[:, hq, eb0 * 128:(eb0 + nmask) * 128], op=ALU.mult),
                              r=["EB"], w=[pres])

                def SC(k):
                    d_ = its[k]
                    hq, kts = d_["hq"], d_["kts"]
                    hk = 0 if kind == "A" else hq
                    hglob = gi * nq + hq
                    n = len(kts)
                    obk = 4 + (k % 2)
                    pb_ = Pb[k % 2]
                    pres = "Pb%d" % (k % 2)
                    if kind == "A":
                        E("pe", lambda e: e.matmul(PS[obk][:, 0:65], onesb[0:1, :], srow[0:1, li, hglob, :], start=True, stop=False),
                          r=["srow", "onesb"], w=["ps%d" % obk])
                    for idx, kt in enumerate(kts):
                        E("pe", lambda e: e.matmul(PS[obk][:, 0:65], pb_[:, idx * 128:(idx + 1) * 128], V[:, kt, hk, :],
                                                   start=(idx == 0 and kind != "A"), stop=(idx == n - 1)),
                          r=[pres, "V%d" % kt], w=["ps%d" % obk])

                def SD(k):
                    d_ = its[k]
                    m, hq = d_["m"], d_["hq"]
                    obk = 4 + (k % 2)
                    smt = sm[k % 2]
                    smres = "sm%d" % (k % 2)
                    ogb = OGb[m % 2]
                    ogres = "OGb%d" % (m % 2)
                    E("dve", lambda e: e.reciprocal(out=smt[:, 1:2], in_=PS[obk][:, 64:65]), w=["ps%d" % obk, smres])
                    E("dve", lambda e: e.scalar_tensor_tensor(out=ogb[:, hq * 64:(hq + 1) * 64], in0=PS[obk][:, 0:64], scalar=smt[:, 1:2],
                                                              in1=SG[:, m, hq * 64:(hq + 1) * 64], op0=ALU.mult, op1=ALU.mult),
                      r=[smres, "SG%d" % m], w=["ps%d" % obk, ogres])

                def OGTR_pe(m):
                    ogb = OGb[m % 2]
                    ogres = "OGb%d" % (m % 2)
                    nch = nq // 2
                    pto = PS[6][:].bitcast(BF16)
                    for c_ in range(nch):
                        E("pe", lambda e: e.transpose(pto[:, c_ * 128:(c_ + 1) * 128], ogb[:, c_ * 128:(c_ + 1) * 128], idb[:]),
                          r=[ogres, "idb"], w=["ps6"])

                def OGTR_act(m):
                    nch = nq // 2
                    pto = PS[6][:].bitcast(BF16)
                    ch0 = gi * nch
                    E("act", lambda e: e.activation(out=OGT[:, ch0:ch0 + nch, m * 128:(m + 1) * 128],
                                                    in_=pto[:, 0:nch * 128].rearrange("p (c n) -> p c n", c=nch), func=AF.Copy),
                      w=["ps6", "OGT%d" % m])

                pend = None
                pend2 = None
                SA(0)
                for k in range(len(its) + 1):
                    if k + 1 < len(its):
                        SA(k + 1)
                    if k < len(its):
                        SB(k)
                    if pend2 is not None:
                        OGTR_act(pend2)
                        pend2 = None
                    if k < len(its):
                        SC(k)
                    if pend is not None:
                        OGTR_pe(pend)
                        pend2 = pend
                        pend = None
                    if k > 0:
                        SD(k - 1)
                        if its[k - 1]["hq"] == nq - 1:
                            pend = its[k - 1]["m"]
                if pend is not None:
                    OGTR_pe(pend)
                    pend2 = pend
                if pend2 is not None:
                    OGTR_act(pend2)

            p.barrier()
            p.tag = "outp%s" % kind
            E("sp", lambda e: e.dma_start(out=lng[:], in_=ln_g[i].partition_broadcast(128)), w=["lng"], dma="lng")
            E("sp", lambda e: e.dma_start(out=lnb[:], in_=ln_b[i].partition_broadcast(128)), w=["lnb"], dma="lnb")
            E("sp", lambda e: e.dma_start(out=gate[:], in_=gscr[i, b].partition_broadcast(128)), r=["gscr"], w=["gate"], dma="gate")
            E("sp", lambda e: e.dma_start(out=gatec[:], in_=gscr[i, 4].partition_broadcast(128)), r=["gscr"], w=["gatec"], dma="gatec")
            NOT = 18 if ctx_out else 16

            def load_hin(t):
                E("sp", lambda e: e.dma_start(out=hin[t % 2][:], in_=tok_src(b, i, t)), r=[hres(i - 1, t)] if i > 0 else [],
                  w=["hin%d" % (t % 2)], dma="hin%d" % (t % 2))

            def OMM(t):
                yb = 2 * (t % 2)
                for half in range(2):
                    for c in range(8):
                        E("pe", lambda e: e.matmul(PS[yb + half][:, :], OGT[:, c, t * 128:(t + 1) * 128], wo[:, c, half * 512:(half + 1) * 512],
                                                   start=(c == 0), stop=(c == 7)), r=["OGT%d" % t, "wo"], w=["ps%d" % (yb + half)])

            def bufs(t):
                return dict(yb=2 * (t % 2), gt_=gate if t < 16 else gatec, gres="gate" if t < 16 else "gatec", hi=hin[t % 2], ho=hout[t % 2],
                            ls=lst[t % 2], lres="lst%d" % (t % 2), tA=tmpA if t % 2 == 0 else tmpA2, tB=tmpB if t % 2 == 0 else tmpB2,
                            rA="tmpA%d" % (t % 2), rB="tmpB%d" % (t % 2), hor="hout%d" % (t % 2), hir="hin%d" % (t % 2))

            def LN1(t):
                B_ = bufs(t)
                yb, gt_, gres, hi, ls, lres, tA, tB, rA, rB = (B_[k_] for k_ in ["yb", "gt_", "gres", "hi", "ls", "lres", "tA", "tB", "rA", "rB"])
                for half in range(2):
                    sl = slice(half * 512, (half + 1) * 512)
                    E("dve", lambda e: e.tensor_tensor(out=tA[:, sl], in0=PS[yb + half][:, :], in1=gt_[:, sl], op=ALU.mult),
                      r=[gres], w=["ps%d" % (yb + half), rA])
                E("dve", lambda e: e.scalar_tensor_tensor(out=tB[:], in0=hi[:], scalar=ALPHA, in1=tA[:], op0=ALU.mult, op1=ALU.add),
                  r=[B_["hir"], rA], w=[rB])
                E("dve", lambda e: e.bn_stats(out=ls[:, 0:6], in_=tB[:, 0:512]), r=[rB], w=[lres])
                E("dve", lambda e: e.bn_stats(out=ls[:, 6:12], in_=tB[:, 512:1024]), r=[rB], w=[lres])
                E("dve", lambda e: e.bn_aggr(out=ls[:, 12:14], in_=ls[:, 0:12]), w=[lres])
                E("dve", lambda e: e.tensor_scalar(out=ls[:, 14:15], in0=ls[:, 13:14], scalar1=EPS, scalar2=None, op0=ALU.add), w=[lres])
                E("pool", lambda e: e.tensor_tensor(out=ls[:, 15:16], in0=ls[:, 14:15], in1=mhalf[:], op=ALU.pow), r=["mhalf"], w=[lres])

            def LN2a(t):
                B_ = bufs(t)
                ls, lres, tA, tB, rA, rB = (B_[k_] for k_ in ["ls", "lres", "tA", "tB", "rA", "rB"])
                E("dve", lambda e: e.scalar_tensor_tensor(out=ls[:, 14:15], in0=ls[:, 12:13], scalar=-1.0, in1=ls[:, 15:16], op0=ALU.mult, op1=ALU.mult),
                  w=[lres])
                E("act", lambda e: e.activation(out=tA[:], in_=tB[:], func=AF.Identity, scale=ls[:, 15:16], bias=ls[:, 14:15]),
                  r=[lres, rB], w=[rA])

            def LN2b(t):
                B_ = bufs(t)
                ho, tA, tB, rA, rB, hor = (B_[k_] for k_ in ["ho", "tA", "tB", "rA", "rB", "hor"])
                E("dve", lambda e: e.tensor_tensor(out=tB[:], in0=tA[:], in1=lng[:], op=ALU.mult), r=[rA, "lng"], w=[rB])
                E("pool", lambda e: e.tensor_tensor(out=ho[:], in0=tB[:], in1=lnb[:], op=ALU.add), r=[rB, "lnb"], w=[hor])
                if last:
                    dst = out[b, t * 128:(t + 1) * 128, :]
                    E("sp", lambda e: e.dma_start(out=dst, in_=ho[:]), r=[hor], w=["out"], dma=hor)
                else:
                    dst = hscr[i % 2, t * 128:(t + 1) * 128, :]
                    E("sp", lambda e: e.dma_start(out=dst, in_=ho[:]), r=[hor], w=[hres(i, t)], dma=hor)

            load_hin(0)
            OMM(0)
            for t in range(NOT + 3):
                if 0 <= t - 2 < NOT:
                    LN2b(t - 2)
                if t + 1 < NOT:
                    load_hin(t + 1)
                    OMM(t + 1)
                if t < NOT:
                    LN1(t)
                if 0 <= t - 1 < NOT:
                    LN2a(t - 1)
                if 0 <= t - 3 < NOT and not last:
                    make_uT(b, i + 1, t - 3, hout[(t - 3) % 2], "hout%d" % ((t - 3) % 2))
            p.barrier()
            E("dve", lambda e: e.memset(V[:], 2.0), w=["V%d" % t for t in range(NT)])
    p.barrier()
    nc._pe_tags = p.pe_tags
    return nc


def _rope_tables():
    t = np.arange(S)
    row = (t // 64).astype(np.float32)
    col = (t % 64).astype(np.float32)
    inv = (10000.0 ** (-np.arange(16, dtype=np.float32) / 16)).astype(np.float32)
    ar = row[:, None] * inv[None]
    ac = col[:, None] * inv[None]
    cx = np.concatenate([np.cos(ar), np.cos(ar), np.cos(ac), np.cos(ac)], axis=1)
    sx = np.concatenate([-np.sin(ar), np.sin(ar), -np.sin(ac), np.sin(ac)], axis=1)
    cx = cx.reshape(16, 128, 64).transpose(1, 0, 2).reshape(128, 16 * 64)
    sx = sx.reshape(16, 128, 64).transpose(1, 0, 2).reshape(128, 16 * 64)
    return np.ascontiguousarray(cx, np.float32), np.ascontiguousarray(sx, np.float32)


_B_CLASSES = [(0, [0, 1, 2, 3]), (1, [0, 1, 2, 3]), (5, [3, 4, 5, 6, 7]), (14, [12, 13, 14, 15]), (15, [12, 13, 14, 15])]


def _b_tables():
    ridx = np.zeros((128, 21, 128), np.int64)
    cidx = np.zeros((128, 21, 128), np.int64)
    allowed = np.zeros((128, 21, 128), bool)
    key = np.arange(128)
    rk, ck = key // 64, key % 64
    rq, cq = key // 64, key % 64
    t = 0
    for m, kts in _B_CLASSES:
        for kt in kts:
            r = 2 * m + rq[None, :]
            kr = 2 * kt + rk[:, None]
            rs = np.clip(r - 4, 0, 24)
            cs = np.clip(cq[None, :] - 8, 0, 48)
            ok = (kr >= rs) & (kr <= rs + 7) & (ck[:, None] >= cs) & (ck[:, None] <= cs + 15)
            ridx[:, t, :] = np.clip(kr - r + 7, 0, 14)
            cidx[:, t, :] = np.clip(ck[:, None] - cq[None, :] + 15, 0, 30)
            allowed[:, t, :] = ok
            t += 1
    return ridx, cidx, allowed


def _prep_shared(inp):
    f = np.float32
    sh = {}
    sh["w_ada"] = np.ascontiguousarray(inp["w_ada"], f)
    b_ada = np.asarray(inp["b_ada"], f)
    sh["badaT"] = np.ascontiguousarray(b_ada[:, :2048].reshape(DEPTH, 16, 128).transpose(2, 0, 1))
    sh["bgate"] = np.ascontiguousarray(b_ada[:, 2048:])
    sh["ln_g"] = np.ascontiguousarray(inp["ln_g"], f)
    sh["ln_b"] = np.ascontiguousarray(inp["ln_b"], f)
    aw = np.asarray(inp["a_w_in"], f)
    wA = np.zeros((2, 4, 128, 8, 640), f)
    for j in range(4):
        cols = np.concatenate([np.arange(256 * j, 256 * j + 256), 1024 + 64 * j + np.arange(64), 1280 + 64 * j + np.arange(64),
                               1536 + 256 * j + np.arange(256)])
        wA[:, j] = aw[:, :, cols].reshape(2, 8, 128, 640).transpose(0, 2, 1, 3)
    sh["wA"] = wA.reshape(2, 4, 128, 8 * 640)
    bw = np.asarray(inp["b_w_in"], f)
    wB = np.zeros((2, 8, 128, 8, 512), f)
    for pp in range(8):
        cols = np.concatenate([k * 1024 + 128 * pp + np.arange(128) for k in range(4)])
        wB[:, pp] = bw[:, :, cols].reshape(2, 8, 128, 512).transpose(0, 2, 1, 3)
    sh["wB"] = wB.reshape(2, 8, 128, 8 * 512)
    sh["woA"] = np.ascontiguousarray(np.asarray(inp["a_w_out"], f).reshape(2, 8, 128, D).transpose(0, 2, 1, 3)).reshape(2, 128, 8 * D)
    sh["woB"] = np.ascontiguousarray(np.asarray(inp["b_w_out"], f).reshape(2, 8, 128, D).transpose(0, 2, 1, 3)).reshape(2, 128, 8 * D)
    sh["sinkA"] = np.ascontiguousarray(inp["a_sink"], f)
    ridx, cidx, allowed = _b_tables()
    rb = np.asarray(inp["b_rel_bias"], f)
    sh["biasG"] = np.ascontiguousarray(rb[:, :, ridx, cidx]).reshape(2, 16, 128, 21 * 128)
    sh["maskB"] = np.where(allowed, 0.0, -30000.0).astype(f).reshape(128, 21 * 128)
    kp = np.arange(128)[:, None]
    qf = np.arange(128)[None, :]
    sh["maskA"] = np.concatenate([(kp >= qf), (kp <= qf)], axis=1).astype(f)
    sh["cosx"], sh["sinx"] = _rope_tables()
    sh["ident"] = np.eye(128, dtype=f)
    return sh


_NC_CACHE = {}


def kernel(x, c, ctx, c_ctx, w_ada, b_ada, ln_g, ln_b, a_w_in, a_w_out, a_sink, b_w_in, b_w_out, b_rel_bias):
    inp = dict(w_ada=w_ada, b_ada=b_ada, ln_g=ln_g, ln_b=ln_b, a_w_in=a_w_in, a_w_out=a_w_out, a_sink=a_sink,
               b_w_in=b_w_in, b_w_out=b_w_out, b_rel_bias=b_rel_bias)
    sh = _prep_shared(inp)
    x = np.asarray(x, np.float32)
    ctx = np.asarray(ctx, np.float32)
    c = np.asarray(c, np.float32)
    c_ctx = np.asarray(c_ctx, np.float32)
    NB = x.shape[0] // NCORES
    if NB not in _NC_CACHE:
        _NC_CACHE[NB] = build_nc(NB, DEPTH)
    nc = _NC_CACHE[NB]
    in_maps = []
    for core in range(NCORES):
        sl = slice(core * NB, (core + 1) * NB)
        cc = np.concatenate([c[sl], np.broadcast_to(c_ctx[None], (5 - NB, D))], axis=0)[:5] if NB < 5 else None
        cc5 = np.zeros((5, D), np.float32)
        cc5[:NB] = c[sl]
        cc5[4] = c_ctx
        m = dict(sh)
        m["x"] = np.ascontiguousarray(x[sl])
        m["ctx"] = np.ascontiguousarray(ctx[sl])
        m["cT"] = np.ascontiguousarray(cc5.reshape(5, 8, 128).transpose(2, 1, 0))
        in_maps.append(m)
    res = run_bass_kernel_spmd(nc, in_maps, core_ids=list(range(NCORES)))
    return np.concatenate([r["out"] for r in res.results], axis=0)
```

```python
import numpy as np
import concourse.bass as bass
import concourse.mybir as mybir
from concourse.bass_utils import run_bass_kernel_spmd

F32 = mybir.dt.float32
BF16 = mybir.dt.bfloat16
AF = mybir.ActivationFunctionType
ALU = mybir.AluOpType

D = 1024
S = 2048
C = 256
NT = 18
DEPTH = 4
ALPHA = (2.0 * DEPTH) ** 0.25
EPS = 1e-5
NCORES = 8
WARM = 0


class P:
    def __init__(self, nc):
        self.nc = nc
        self.eng = {"pe": nc.tensor, "act": nc.scalar, "dve": nc.vector, "pool": nc.gpsimd, "sp": nc.sync}
        self.sems = {}
        self.cnt = {}
        for k in ["pe", "act", "dve", "pool"]:
            self.sems[k] = nc.alloc_semaphore("e_" + k)
            self.cnt[k] = 0
        self.waited = {k: {} for k in self.eng}
        self.lastw = {}
        self.readers = {}
        self.tag = ""
        self.pe_tags = []

    def _need(self, e, toks):
        for (s, v) in toks:
            if v <= 0:
                continue
            if s == "pe" and e == "pe":
                continue
            if self.waited[e].get(s, 0) >= v:
                continue
            self.eng[e].wait_ge(self.sems[s], v)
            self.waited[e][s] = v

    def emit(self, e, fn, r=(), w=(), dma=None):
        toks = []
        for res in r:
            if res in self.lastw:
                toks.append(self.lastw[res])
        for res in w:
            if res in self.lastw:
                toks.append(self.lastw[res])
            for s, v in self.readers.get(res, {}).items():
                toks.append((s, v))
        self._need(e, toks)
        ins = fn(self.eng[e])
        if e == "pe":
            self.pe_tags.append(self.tag)
        if dma is None:
            self.cnt[e] += 1
            ins.then_inc(self.sems[e], 1)
            tok = (e, self.cnt[e])
        else:
            if dma not in self.sems:
                self.sems[dma] = self.nc.alloc_semaphore("d_" + dma)
                self.cnt[dma] = 0
            self.cnt[dma] += 16
            ins.then_inc(self.sems[dma], 16)
            tok = (dma, self.cnt[dma])
        for res in w:
            self.lastw[res] = tok
            self.readers[res] = {}
        for res in r:
            d = self.readers.setdefault(res, {})
            d[tok[0]] = max(d.get(tok[0], 0), tok[1])
        return tok

    def barrier(self):
        toks = [(s, v) for s, v in self.cnt.items()]
        for e in self.eng:
            self._need(e, toks)


def _layer_cfg(i):
    if i % 2 == 0:
        return dict(kind="A", NG=4, nq=4, nkv=1, XW=320, YW=320, GW=640)
    return dict(kind="B", NG=8, nq=2, nkv=2, XW=256, YW=256, GW=512)


def build_nc(NB=4, NL=DEPTH):
    nc = bass.Bass("TRN2", target_bir_lowering=False)
    dt = nc.dram_tensor
    x = dt("x", [NB, S, D], F32, kind="ExternalInput").ap()
    ctx = dt("ctx", [NB, C, D], F32, kind="ExternalInput").ap()
    out = dt("out", [NB, S, D], F32, kind="ExternalOutput").ap()
    cT = dt("cT", [128, 8, 5], F32, kind="ExternalInput").ap()
    w_ada = dt("w_ada", [DEPTH, D, 3 * D], F32, kind="ExternalInput").ap()
    badaT = dt("badaT", [128, DEPTH, 16], F32, kind="ExternalInput").ap()
    bgate = dt("bgate", [DEPTH, D], F32, kind="ExternalInput").ap()
    ln_g = dt("ln_g", [DEPTH, D], F32, kind="ExternalInput").ap()
    ln_b = dt("ln_b", [DEPTH, D], F32, kind="ExternalInput").ap()
    lnT_d = dt("lnT", [128, 2 * DEPTH * 8], F32, kind="ExternalInput").ap()
    wA = dt("wA", [2, 4, 128, 8 * 640], F32, kind="ExternalInput").ap()
    wB = dt("wB", [2, 8, 128, 8 * 512], F32, kind="ExternalInput").ap()
    woA = dt("woA", [2, 128, 8 * D], F32, kind="ExternalInput").ap()
    woB = dt("woB", [2, 128, 8 * D], F32, kind="ExternalInput").ap()
    sinkA = dt("sinkA", [2, 16], F32, kind="ExternalInput").ap()
    biasG = dt("biasG", [2, 16, 128, 21 * 128], F32, kind="ExternalInput").ap()
    maskB_d = dt("maskB", [128, 21 * 128], F32, kind="ExternalInput").ap()
    maskA_d = dt("maskA", [128, 256], F32, kind="ExternalInput").ap()
    cosx_d = dt("cosx", [128, 16 * 64], F32, kind="ExternalInput").ap()
    sinx_d = dt("sinx", [128, 16 * 64], F32, kind="ExternalInput").ap()
    ident_d = dt("ident", [128, 128], F32, kind="ExternalInput").ap()
    hscr = dt("hscr", [2, S + C, D], F32).ap()
    gscr = dt("gscr", [DEPTH, 5, D], F32).ap()

    sb = nc.alloc_sbuf_tensor
    uT = sb("uT", [128, 8, NT * 128], BF16)
    OGT = sb("OGT", [128, 8, NT * 128], BF16)
    wg = [sb("wg%d" % i, [128, 8, 640], BF16) for i in range(2)]
    wo = sb("wo", [128, 8, D], BF16)
    qk_tm = [sb("qktm%d" % i, [128, 320], BF16) for i in range(2)]
    Pb = [sb("Pb%d" % i, [128, 896], BF16) for i in range(2)]
    EB = sb("EB", [128, 2, 21 * 128], BF16)
    ebraw = sb("ebraw", [128, 11 * 128], F32)
    maskB = sb("maskBs", [128, 21 * 128], BF16)
    maskA = sb("maskAs", [128, 256], BF16)
    cosx = sb("cosxs", [128, 16, 64], F32)
    sinx = sb("sinxs", [128, 16, 64], F32)
    idf = sb("idf", [128, 128], F32)
    idb = sb("idb", [128, 128], BF16)
    modT = sb("modT", [128, DEPTH, 16, 5], F32)
    bada = sb("badas", [128, DEPTH, 16], F32)
    cTs = sb("cTs", [128, 8, 5], F32)
    s2 = sb("s2", [128, 8, 5], F32)
    esink = sb("esink", [128, 16], F32)
    OGb = [sb("OGb%d" % i, [128, 256], BF16) for i in range(2)]
    th2 = [sb("th%d" % i, [128, 256], BF16) for i in range(2)]
    tmpA2 = sb("tmpA2", [128, D], F32)
    tmpB2 = sb("tmpB2", [128, D], F32)
    rt1 = sb("rt1", [128, 320], F32)
    rt2 = sb("rt2", [128, 320], F32)
    sm = [sb("sm%d" % i, [128, 8], F32) for i in range(2)]
    lst = [sb("lst%d" % i, [128, 16], F32) for i in range(2)]
    mhalf = sb("mhalf", [128, 1], F32)
    lnT = sb("lnTs", [128, 2, DEPTH, 8], F32)
    cmb = sb("cmb", [128, 2, 2, 8], F32)
    onesb = sb("onesb", [1, 128], BF16)
    srow = sb("srow", [1, 2, 16, 65], BF16)
    sinkst = sb("sinkst", [1, 32], F32)
    roff = (nc.sbuf_base + 31) // 32 * 32
    Rbuf = sb("Rbuf", [128, 41600 + 64], mybir.dt.uint8)
    assert nc.sbuf_base >= roff + 41600, (nc.sbuf_base, roff)

    def at(name, shape, dtype, off):
        return nc.alloc_sbuf_tensor_at(name, shape, dtype, offset=roff + off)

    QT = at("QT", [128, 4, NT * 128], BF16, 0)
    KT = at("KT", [128, 2, NT * 128], BF16, 18432)
    V = at("V", [128, NT, 2, 65], BF16, 27648)
    SG = at("SG", [128, NT, 256], BF16, 32384)
    hin = [at("hin%d" % i, [128, D], F32, 4096 * i) for i in range(2)]
    hout = [at("hout%d" % i, [128, D], F32, 8192 + 4096 * i) for i in range(2)]
    tmpA = at("tmpA", [128, D], F32, 16384)
    tmpB = at("tmpB", [128, D], F32, 20480)
    lng = at("lng", [128, D], F32, 24576)
    lnb = at("lnb", [128, D], F32, 28672)
    gate = at("gate", [128, D], F32, 32768)
    gatec = at("gatec", [128, D], F32, 36864)
    wa = at("wa", [128, 8, 512], F32, 0)
    gt = at("gt", [5, 512], F32, 16384)
    bg5 = at("bg5", [5, D], F32, 20480)

    PSall = nc.alloc_psum_tensor("psall", [128, 8, 512], F32)
    PS = [PSall[:, i, :] for i in range(8)]

    def psb(i, dtype):
        return PS[i][:].bitcast(dtype) if dtype != F32 else PS[i][:]

    p = P(nc)
    E = p.emit

    E("sp", lambda e: e.dma_start(out=idf[:], in_=ident_d[:, :]), w=["idf"], dma="c0")
    E("sp", lambda e: e.dma_start(out=cosx[:].rearrange("p t d -> p (t d)"), in_=cosx_d[:, :]), w=["cosx"], dma="c1")
    E("sp", lambda e: e.dma_start(out=sinx[:].rearrange("p t d -> p (t d)"), in_=sinx_d[:, :]), w=["sinx"], dma="c2")
    E("pool", lambda e: e.dma_start(out=maskB[:], in_=maskB_d[:, :], max_dma_last_dim=4096), w=["maskB"], dma="c3")
    E("pool", lambda e: e.dma_start(out=maskA[:], in_=maskA_d[:, :]), w=["maskA"], dma="c4")
    E("sp", lambda e: e.dma_start(out=cTs[:].rearrange("p k j -> p (k j)"), in_=cT.rearrange("p k j -> p (k j)")), w=["cTs"], dma="c5")
    E("sp", lambda e: e.dma_start(out=bada[:].rearrange("p l c -> p (l c)"), in_=badaT.rearrange("p l c -> p (l c)")), w=["bada"], dma="c6")
    E("dve", lambda e: e.tensor_copy(out=idb[:], in_=idf[:]), r=["idf"], w=["idb"])
    E("sp", lambda e: e.dma_start(out=lnT[:].rearrange("p a l c -> p (a l c)"), in_=lnT_d[:, :]), w=["lnT"], dma="c7")
    E("dve", lambda e: e.memset(mhalf[:], -0.5), w=["mhalf"])
    E("dve", lambda e: e.memset(onesb[:], 1.0), w=["onesb"])
    E("dve", lambda e: e.memset(srow[:], 0.0), w=["srow"])
    E("sp", lambda e: e.dma_start(out=sinkst[:], in_=sinkA.rearrange("l h -> (l h)").partition_broadcast(1)), w=["sinkst"], dma="esink")
    E("act", lambda e: e.activation(out=sinkst[:], in_=sinkst[:], func=AF.Exp), w=["sinkst"])
    for l_ in range(2):
        E("dve", lambda e: e.tensor_scalar(out=srow[0:1, l_, :, 64], in0=sinkst[0:1, l_ * 16:(l_ + 1) * 16], scalar1=2.0, scalar2=None, op0=ALU.mult),
          r=["sinkst"], w=["srow"])
    E("dve", lambda e: e.tensor_scalar(out=bada[:, :, 8:16], in0=bada[:, :, 8:16], scalar1=1.0, scalar2=None, op0=ALU.add), r=["bada"], w=["bada"])
    E("act", lambda e: e.activation(out=s2[:], in_=cTs[:], func=AF.Tanh, scale=0.5), r=["cTs"], w=["s2"])
    E("dve", lambda e: e.scalar_tensor_tensor(out=s2[:], in0=s2[:], scalar=1.0, in1=cTs[:], op0=ALU.add, op1=ALU.mult), r=["s2", "cTs"], w=["s2"])

    for l in range(NL):
        E("sp", lambda e: e.dma_start(out=bg5[:], in_=bgate[l].partition_broadcast(5)), w=["bg5"], dma="bg5")
        for blk in range(6):
            src = w_ada[l, :, blk * 512:(blk + 1) * 512].rearrange("(k p) n -> p k n", p=128)
            E("sp", lambda e: e.dma_start(out=wa[:], in_=src), w=["wa"], dma="wa")
            if blk < 4:
                for cc in range(4):
                    nch = blk * 4 + cc
                    for k in range(8):
                        E("pe", lambda e: e.matmul(PS[7][:, cc * 8:cc * 8 + 5], wa[:, k, cc * 128:(cc + 1) * 128], s2[:, k, :],
                                                   start=(k == 0), stop=(k == 7)), r=["wa", "s2"], w=["ps7"])
                    E("act", lambda e: e.activation(out=modT[:, l, nch, :], in_=PS[7][:, cc * 8:cc * 8 + 5], func=AF.Identity,
                                                    scale=0.5, bias=bada[:, l, nch:nch + 1]), r=["bada"], w=["ps7", "modT"])
            else:
                for k in range(8):
                    E("pe", lambda e: e.matmul(PS[7][0:5, :], s2[:, k, :], wa[:, k, :], start=(k == 0), stop=(k == 7)),
                      r=["wa", "s2"], w=["ps7"])
                c0 = (blk - 4) * 512
                E("act", lambda e: e.activation(out=gt[:], in_=PS[7][0:5, :], func=AF.Copy, scale=0.5), w=["ps7", "gt"])
                E("dve", lambda e: e.tensor_tensor(out=gt[:], in0=gt[:], in1=bg5[:, c0:c0 + 512], op=ALU.add), r=["bg5"], w=["gt"])
                E("sp", lambda e: e.dma_start(out=gscr[l, :, c0:c0 + 512], in_=gt[:]), r=["gt"], w=["gscr"], dma="gs")
    p.barrier()

    def tok_src(b, i, t):
        if i == 0:
            if t < 16:
                return x[b, t * 128:(t + 1) * 128, :]
            return ctx[b, (t - 16) * 128:(t - 15) * 128, :]
        return hscr[(i - 1) % 2, t * 128:(t + 1) * 128, :]

    def hres(i, t):
        return "h%d_%d" % (i, t)

    def make_uT(b, i, t, src_tile, src_res, comb=False):
        j = b if t < 16 else 4
        v_ = 0 if t < 16 else 1
        pb = 4 + 2 * (t % 2)
        for c in range(8):
            bank = pb + c // 4
            E("pe", lambda e: e.transpose(PS[bank][:, (c % 4) * 128:(c % 4 + 1) * 128], src_tile[:, c * 128:(c + 1) * 128], idf[:]),
              r=[src_res, "idf"], w=["ps%d" % bank])
        for c in range(8):
            bank = pb + c // 4
            sc_ = cmb[:, v_, 0, c:c + 1] if comb else modT[:, i, 8 + c, j:j + 1]
            bi_ = cmb[:, v_, 1, c:c + 1] if comb else modT[:, i, c, j:j + 1]
            E("act", lambda e: e.activation(out=uT[:, c, t * 128:(t + 1) * 128], in_=PS[bank][:, (c % 4) * 128:(c % 4 + 1) * 128],
                                            func=AF.Identity, scale=sc_, bias=bi_),
              r=["cmb" if comb else "modT"], w=["ps%d" % bank, "uT%d" % t])

    E("dve", lambda e: e.memset(V[:], 2.0), w=["V%d" % t for t in range(NT)])
    E("pool", lambda e: e.memset(QT[64:128, :, :], 0.0), w=["QT%d" % t for t in range(NT)])
    E("pool", lambda e: e.memset(KT[64:128, :, :], 0.0), w=["KT%d" % t for t in range(NT)])

    for b in range(NB):
        p.barrier()
        p.tag = "init"
        for t in range(NT):
            hb = hin[t % 2]
            E("sp", lambda e: e.dma_start(out=hb[:], in_=tok_src(b, 0, t)), w=["hin%d" % (t % 2)], dma="hin%d" % (t % 2))
            make_uT(b, 0, t, hb, "hin%d" % (t % 2))
        p.barrier()
        E("dve", lambda e: e.memset(V[:], 2.0), w=["V%d" % t for t in range(NT)])
        E("pool", lambda e: e.memset(QT[64:128, :, :], 0.0), w=["QT%d" % t for t in range(NT)])
        E("pool", lambda e: e.memset(KT[64:128, :, :], 0.0), w=["KT%d" % t for t in range(NT)])

        for i in range(NL):
            cfg = _layer_cfg(i)
            kind, NG, nq, nkv, XW, YW, GW = (cfg[k] for k in ["kind", "NG", "nq", "nkv", "XW", "YW", "GW"])
            li = i // 2
            ctx_out = i < NL - 1
            last = i == NL - 1
            wsrc = wA if kind == "A" else wB
            wosrc = woA if kind == "A" else woB
            NQT = 18 if ctx_out else 16

            def load_wg(gi, ii=i):
                c_ = _layer_cfg(ii)
                src_ = (wA if c_["kind"] == "A" else wB)[ii // 2, gi]
                dst = wg[gi % 2][:, :, 0:c_["GW"]]
                E("pool", lambda e: e.dma_start(out=dst, in_=src_.rearrange("p (k n) -> p k n", k=8), max_dma_last_dim=4096),
                  w=["wg%d" % (gi % 2)], dma="wg%d" % (gi % 2))

            if b == 0 and i == 0:
                load_wg(0)
            for gi in range(NG):
                if gi + 1 < NG:
                    load_wg(gi + 1)
                    if gi == 0:
                        E("pool", lambda e: e.dma_start(out=wo[:].rearrange("p c n -> p (c n)"), in_=wosrc[li], max_dma_last_dim=4096), w=["wo"], dma="wo")
                elif not (b == NB - 1 and i == NL - 1):
                    load_wg(0, (i + 1) % NL)
                wgt = wg[gi % 2]
                wres = "wg%d" % (gi % 2)
                eb_jobs = []
                if kind == "B":
                    for hh in range(2):
                        for (c0, c1) in [(0, 11 * 128), (11 * 128, 21 * 128)]:
                            eb_jobs.append((hh, c0, c1))

                def EBDMA(j):
                    hh, c0, c1 = eb_jobs[j]
                    E("sp", lambda e: e.dma_start(out=ebraw[:, 0:c1 - c0], in_=biasG[li, gi * 2 + hh, :, c0:c1]), w=["ebraw"], dma="ebraw")

                def EBPROC(j):
                    hh, c0, c1 = eb_jobs[j]
                    E("dve", lambda e: e.tensor_tensor(out=ebraw[:, 0:c1 - c0], in0=ebraw[:, 0:c1 - c0], in1=maskB[:, c0:c1], op=ALU.add),
                      r=["maskB"], w=["ebraw"])
                    E("act", lambda e: e.activation(out=EB[:, hh, c0:c1], in_=ebraw[:, 0:c1 - c0], func=AF.Exp), r=["ebraw"], w=["EB"])

                p.tag = "proj%s" % kind
                nh = XW // 64
                gw = nq * 64

                def MM(t):
                    bx = 2 * (t % 2)
                    by = bx + 1
                    for k in range(8):
                        E("pe", lambda e: e.matmul(PS[bx][:, 0:XW], uT[:, k, t * 128:(t + 1) * 128], wgt[:, k, 0:XW], start=(k == 0), stop=(k == 7)),
                          r=["uT%d" % t, wres], w=["ps%d" % bx])
                    for k in range(8):
                        E("pe", lambda e: e.matmul(PS[by][:, 0:YW], uT[:, k, t * 128:(t + 1) * 128], wgt[:, k, XW:XW + YW], start=(k == 0), stop=(k == 7)),
                          r=["uT%d" % t, wres], w=["ps%d" % by])

                def EV(t):
                    bx = 2 * (t % 2)
                    by = bx + 1
                    qt = qk_tm[t % 2]
                    qres = "qktm%d" % (t % 2)
                    if kind == "A" and t < 16:
                        NH = 5
                        xin = PS[bx][:, 0:320].rearrange("p (h d) -> p h d", h=NH)
                        cb = cosx[:, t, :]
                        cb = bass.AP(cb.tensor, cb.offset, [list(cb.ap[0]), [0, NH], [1, 64]])
                        E("dve", lambda e: e.tensor_tensor(out=rt1[:].rearrange("p (h d) -> p h d", h=NH), in0=xin, in1=cb, op=ALU.mult),
                          r=["cosx"], w=["ps%d" % bx, "rt1"])
                        sbp = sinx[:, t, :]
                        for a_ in range(2):
                            sa = bass.AP(sbp.tensor, sbp.offset + a_ * 16, [list(sbp.ap[0]), [0, NH], [32, 2], [1, 16]])
                            xa = PS[bx][:, 0:320].rearrange("p (h r a f) -> p h r a f", h=NH, r=2, a=2, f=16)[:, :, :, 1 - a_, :]
                            ra = rt2[:].rearrange("p (h r a f) -> p h r a f", h=NH, r=2, a=2, f=16)[:, :, :, a_, :]
                            E("dve", lambda e: e.tensor_tensor(out=ra, in0=xa, in1=sa, op=ALU.mult), r=["sinx"], w=["ps%d" % bx, "rt2"])
                        E("pool", lambda e: e.tensor_tensor(out=qt[:, 0:320], in0=rt1[:], in1=rt2[:], op=ALU.add), r=["rt1", "rt2"], w=[qres])
                    else:
                        E("dve", lambda e: e.tensor_copy(out=qt[:, 0:XW], in_=PS[bx][:, 0:XW]), w=["ps%d" % bx, qres])
                    E("act", lambda e: e.activation(out=V[:, t, 0:nkv, 0:64], in_=PS[by][:, 0:nkv * 64].rearrange("p (h d) -> p h d", h=nkv), func=AF.Copy),
                      w=["ps%d" % by, "V%d" % t])
                    th = th2[t % 2]
                    E("act", lambda e: e.activation(out=th[:, 0:gw], in_=PS[by][:, nkv * 64:nkv * 64 + gw], func=AF.Tanh, scale=0.5),
                      w=["ps%d" % by, "th%d" % (t % 2)])
                    E("dve", lambda e: e.scalar_tensor_tensor(out=SG[:, t, 0:gw], in0=th[:, 0:gw], scalar=1.0, in1=PS[by][:, nkv * 64:nkv * 64 + gw],
                                                              op0=ALU.add, op1=ALU.mult), r=["th%d" % (t % 2)], w=["ps%d" % by, "SG%d" % t])

                def TR(t):
                    qt = qk_tm[t % 2]
                    qres = "qktm%d" % (t % 2)
                    tb = 4 + (t % 2)
                    ptr = PS[tb][:].bitcast(BF16)
                    for hh in range(nh):
                        E("pe", lambda e: e.transpose(ptr[0:64, hh * 128:(hh + 1) * 128], qt[:, hh * 64:(hh + 1) * 64], idb[:]),
                          r=[qres, "idb"], w=["ps%d" % tb])

                def TRE(t):
                    tb = 4 + (t % 2)
                    ptr = PS[tb][:].bitcast(BF16)
                    E("act", lambda e: e.activation(out=QT[0:64, 0:nq, t * 128:(t + 1) * 128],
                                                    in_=ptr[0:64, 0:nq * 128].rearrange("p (h n) -> p h n", h=nq), func=AF.Copy),
                      w=["ps%d" % tb, "QT%d" % t])
                    E("act", lambda e: e.activation(out=KT[0:64, 0:nkv, t * 128:(t + 1) * 128],
                                                    in_=ptr[0:64, nq * 128:(nq + nkv) * 128].rearrange("p (h n) -> p h n", h=nkv), func=AF.Copy),
                      w=["ps%d" % tb, "KT%d" % t])

                if eb_jobs:
                    EBDMA(0)
                MM(0)
                for t in range(NT):
                    if t + 1 < NT:
                        MM(t + 1)
                    EV(t)
                    if t > 0:
                        TRE(t - 1)
                    TR(t)
                    if eb_jobs and t in (3, 7, 11, 15):
                        j = (t - 3) // 4
                        EBPROC(j)
                        if j + 1 < len(eb_jobs):
                            EBDMA(j + 1)
                TRE(NT - 1)

                p.tag = "attn%s" % kind
                its = []
                for m in range(NQT):
                    eb0 = None
                    if m >= 16:
                        kts = [16, 17]
                        nmask = 0
                        mwhich = None
                    elif kind == "A":
                        kts = []
                        mwhich = []
                        if m > 0:
                            kts.append(m - 1)
                            mwhich.append(0)
                        if m < 15:
                            kts.append(m + 1)
                            mwhich.append(1)
                        nmask = len(kts)
                        kts += [m, 16, 17]
                    else:
                        mwhich = None
                        if m == 0:
                            lat, eb0 = [0, 1, 2, 3], 0
                        elif m == 1:
                            lat, eb0 = [0, 1, 2, 3], 4
                        elif m == 14:
                            lat, eb0 = [12, 13, 14, 15], 13
                        elif m == 15:
                            lat, eb0 = [12, 13, 14, 15], 17
                        else:
                            lat, eb0 = [m - 2, m - 1, m, m + 1, m + 2], 8
                        nmask = len(lat)
                        kts = lat + [16, 17]
                    for hq in range(nq):
                        its.append(dict(m=m, hq=hq, kts=kts, nmask=nmask, eb0=eb0, mwhich=mwhich))

                def SA(k):
                    d_ = its[k]
                    m, hq, kts = d_["m"], d_["hq"], d_["kts"]
                    hk = 0 if kind == "A" else hq
                    sbk = 2 * (k % 2)
                    for idx, kt in enumerate(kts):
                        bank = sbk + idx // 4
                        E("pe", lambda e: e.matmul(PS[bank][:, (idx % 4) * 128:(idx % 4 + 1) * 128], KT[:, hk, kt * 128:(kt + 1) * 128],
                                                   QT[:, hq, m * 128:(m + 1) * 128], start=True, stop=True),
                          r=["KT%d" % kt, "QT%d" % m], w=["ps%d" % bank])

                def SB(k):
                    d_ = its[k]
                    hq, kts, nmask, eb0, mwhich = d_["hq"], d_["kts"], d_["nmask"], d_["eb0"], d_["mwhich"]
                    n = len(kts)
                    sbk = 2 * (k % 2)
                    pb_ = Pb[k % 2]
                    pres = "Pb%d" % (k % 2)
                    sview = PSall[:, sbk:sbk + 2, :].rearrange("p b n -> p (b n)")
                    E("act", lambda e: e.activation(out=pb_[:, 0:n * 128], in_=sview[:, 0:n * 128], func=AF.Exp, scale=0.125),
                      w=["ps%d" % sbk, pres] + (["ps%d" % (sbk + 1)] if n > 4 else []))
                    if nmask > 0:
                        if kind == "A":
                            msk = maskA[:, 0:256] if nmask == 2 else maskA[:, mwhich[0] * 128:(mwhich[0] + 1) * 128]
                            E("dve", lambda e: e.tensor_tensor(out=pb_[:, 0:nmask * 128], in0=pb_[:, 0:nmask * 128], in1=msk, op=ALU.mult),
                              r=["maskA"], w=[pres])
                        else:
                            E("dve", lambda e: e.tensor_tensor(out=pb_[:, 0:nmask * 128], in0=pb_[:, 0:nmask * 128],
                                                               in1=EB[:, hq, eb0 * 128:(eb0 + nmask) * 128], op=ALU.mult),
                              r=["EB"], w=[pres])

                def SC(k):
                    d_ = its[k]
                    hq, kts = d_["hq"], d_["kts"]
                    hk = 0 if kind == "A" else hq
                    hglob = gi * nq + hq
                    n = len(kts)
                    obk = 4 + (k % 2)
                    pb_ = Pb[k % 2]
                    pres = "Pb%d" % (k % 2)
                    if kind == "A":
                        E("pe", lambda e: e.matmul(PS[obk][:, 0:65], onesb[0:1, :], srow[0:1, li, hglob, :], start=True, stop=False),
                          r=["srow", "onesb"], w=["ps%d" % obk])
                    for idx, kt in enumerate(kts):
                        E("pe", lambda e: e.matmul(PS[obk][:, 0:65], pb_[:, idx * 128:(idx + 1) * 128], V[:, kt, hk, :],
                                                   start=(idx == 0 and kind != "A"), stop=(idx == n - 1)),
                          r=[pres, "V%d" % kt], w=["ps%d" % obk])

                def SD(k):
                    d_ = its[k]
                    m, hq = d_["m"], d_["hq"]
                    obk = 4 + (k % 2)
                    smt = sm[k % 2]
                    smres = "sm%d" % (k % 2)
                    ogb = OGb[m % 2]
                    ogres = "OGb%d" % (m % 2)
                    E("dve", lambda e: e.reciprocal(out=smt[:, 1:2], in_=PS[obk][:, 64:65]), w=["ps%d" % obk, smres])
                    E("dve", lambda e: e.scalar_tensor_tensor(out=ogb[:, hq * 64:(hq + 1) * 64], in0=PS[obk][:, 0:64], scalar=smt[:, 1:2],
                                                              in1=SG[:, m, hq * 64:(hq + 1) * 64], op0=ALU.mult, op1=ALU.mult),
                      r=[smres, "SG%d" % m], w=["ps%d" % obk, ogres])

                def OGTR_pe(m):
                    ogb = OGb[m % 2]
                    ogres = "OGb%d" % (m % 2)
                    nch = nq // 2
                    pto = PS[6][:].bitcast(BF16)
                    for c_ in range(nch):
                        E("pe", lambda e: e.transpose(pto[:, c_ * 128:(c_ + 1) * 128], ogb[:, c_ * 128:(c_ + 1) * 128], idb[:]),
                          r=[ogres, "idb"], w=["ps6"])

                def OGTR_act(m):
                    nch = nq // 2
                    pto = PS[6][:].bitcast(BF16)
                    ch0 = gi * nch
                    E("act", lambda e: e.activation(out=OGT[:, ch0:ch0 + nch, m * 128:(m + 1) * 128],
                                                    in_=pto[:, 0:nch * 128].rearrange("p (c n) -> p c n", c=nch), func=AF.Copy),
                      w=["ps6", "OGT%d" % m])

                pend = None
                pend2 = None
                SA(0)
                for k in range(len(its) + 1):
                    if k + 1 < len(its):
                        SA(k + 1)
                    if WARM and k < len(its):
                        for _ in range(WARM):
                            E("pe", lambda e: e.matmul(PS[7][:, :], idb[:], uT[:, 0, 0:512], start=True, stop=True),
                              r=["idb", "uT0", "uT1", "uT2", "uT3"], w=["ps7"])
                    if k < len(its):
                        SB(k)
                    if pend2 is not None:
                        OGTR_act(pend2)
                        pend2 = None
                    if k < len(its):
                        SC(k)
                    if pend is not None:
                        OGTR_pe(pend)
                        pend2 = pend
                        pend = None
                    if k > 0:
                        SD(k - 1)
                        if its[k - 1]["hq"] == nq - 1:
                            pend = its[k - 1]["m"]
                if pend is not None:
                    OGTR_pe(pend)
                    pend2 = pend
                if pend2 is not None:
                    OGTR_act(pend2)

            p.barrier()
            p.tag = "outp%s" % kind
            E("sp", lambda e: e.dma_start(out=lng[:], in_=ln_g[i].partition_broadcast(128)), w=["lng"], dma="lng")
            E("sp", lambda e: e.dma_start(out=lnb[:], in_=ln_b[i].partition_broadcast(128)), w=["lnb"], dma="lnb")
            E("sp", lambda e: e.dma_start(out=gate[:], in_=gscr[i, b].partition_broadcast(128)), r=["gscr"], w=["gate"], dma="gate")
            E("sp", lambda e: e.dma_start(out=gatec[:], in_=gscr[i, 4].partition_broadcast(128)), r=["gscr"], w=["gatec"], dma="gatec")
            NOT = 18 if ctx_out else 16
            if not last:
                for v_, j_ in ((0, b), (1, 4)):
                    E("dve", lambda e: e.tensor_tensor(out=cmb[:, v_, 0, :], in0=lnT[:, 0, i, :], in1=modT[:, i + 1, 8:16, j_], op=ALU.mult),
                      r=["lnT", "modT"], w=["cmb"])
                    E("dve", lambda e: e.tensor_tensor(out=cmb[:, v_, 1, :], in0=lnT[:, 1, i, :], in1=modT[:, i + 1, 8:16, j_], op=ALU.mult),
                      r=["lnT", "modT"], w=["cmb"])
                    E("dve", lambda e: e.tensor_tensor(out=cmb[:, v_, 1, :], in0=cmb[:, v_, 1, :], in1=modT[:, i + 1, 0:8, j_], op=ALU.add),
                      r=["modT"], w=["cmb"])

            def load_hin(t):
                E("sp", lambda e: e.dma_start(out=hin[t % 2][:], in_=tok_src(b, i, t)), r=[hres(i - 1, t)] if i > 0 else [],
                  w=["hin%d" % (t % 2)], dma="hin%d" % (t % 2))

            def OMM(t):
                yb = 2 * (t % 2)
                for half in range(2):
                    for c in range(8):
                        E("pe", lambda e: e.matmul(PS[yb + half][:, :], OGT[:, c, t * 128:(t + 1) * 128], wo[:, c, half * 512:(half + 1) * 512],
                                                   start=(c == 0), stop=(c == 7)), r=["OGT%d" % t, "wo"], w=["ps%d" % (yb + half)])

            def bufs(t):
                return dict(yb=2 * (t % 2), gt_=gate if t < 16 else gatec, gres="gate" if t < 16 else "gatec", hi=hin[t % 2], ho=hout[t % 2],
                            ls=lst[t % 2], lres="lst%d" % (t % 2), tA=tmpA if t % 2 == 0 else tmpA2, tB=tmpB if t % 2 == 0 else tmpB2,
                            rA="tmpA%d" % (t % 2), rB="tmpB%d" % (t % 2), hor="hout%d" % (t % 2), hir="hin%d" % (t % 2))

            def LN1(t):
                B_ = bufs(t)
                yb, gt_, gres, hi, ls, lres, tA, tB, rA, rB = (B_[k_] for k_ in ["yb", "gt_", "gres", "hi", "ls", "lres", "tA", "tB", "rA", "rB"])
                yview = PSall[:, yb:yb + 2, :].rearrange("p b n -> p (b n)")
                E("dve", lambda e: e.tensor_tensor(out=tA[:], in0=yview, in1=gt_[:], op=ALU.mult),
                  r=[gres], w=["ps%d" % yb, "ps%d" % (yb + 1), rA])
                E("dve", lambda e: e.scalar_tensor_tensor(out=tB[:], in0=hi[:], scalar=ALPHA, in1=tA[:], op0=ALU.mult, op1=ALU.add),
                  r=[B_["hir"], rA], w=[rB])
                E("dve", lambda e: e.bn_stats(out=ls[:, 0:6], in_=tB[:, 0:512]), r=[rB], w=[lres])
                E("dve", lambda e: e.bn_stats(out=ls[:, 6:12], in_=tB[:, 512:1024]), r=[rB], w=[lres])
                E("dve", lambda e: e.bn_aggr(out=ls[:, 12:14], in_=ls[:, 0:12]), w=[lres])
                E("dve", lambda e: e.tensor_scalar(out=ls[:, 14:15], in0=ls[:, 13:14], scalar1=EPS, scalar2=None, op0=ALU.add), w=[lres])
                E("pool", lambda e: e.tensor_tensor(out=ls[:, 15:16], in0=ls[:, 14:15], in1=mhalf[:], op=ALU.pow), r=["mhalf"], w=[lres])

            def LN2a(t):
                B_ = bufs(t)
                ls, lres, tA, tB, rA, rB = (B_[k_] for k_ in ["ls", "lres", "tA", "tB", "rA", "rB"])
                E("dve", lambda e: e.scalar_tensor_tensor(out=ls[:, 14:15], in0=ls[:, 12:13], scalar=-1.0, in1=ls[:, 15:16], op0=ALU.mult, op1=ALU.mult),
                  w=[lres])
                E("act", lambda e: e.activation(out=tA[:], in_=tB[:], func=AF.Identity, scale=ls[:, 15:16], bias=ls[:, 14:15]),
                  r=[lres, rB], w=[rA])

            def LN2b(t):
                B_ = bufs(t)
                ho, tA, tB, rA, rB, hor = (B_[k_] for k_ in ["ho", "tA", "tB", "rA", "rB", "hor"])
                E("dve", lambda e: e.tensor_tensor(out=tB[:], in0=tA[:], in1=lng[:], op=ALU.mult), r=[rA, "lng"], w=[rB])
                E("pool", lambda e: e.tensor_tensor(out=ho[:], in0=tB[:], in1=lnb[:], op=ALU.add), r=[rB, "lnb"], w=[hor])
                if last:
                    dst = out[b, t * 128:(t + 1) * 128, :]
                    E("sp", lambda e: e.dma_start(out=dst, in_=ho[:]), r=[hor], w=["out"], dma=hor)
                else:
                    dst = hscr[i % 2, t * 128:(t + 1) * 128, :]
                    E("sp", lambda e: e.dma_start(out=dst, in_=ho[:]), r=[hor], w=[hres(i, t)], dma=hor)

            load_hin(0)
            OMM(0)
            for t in range(NOT + 2):
                if 0 <= t - 2 < NOT:
                    LN2b(t - 2)
                    if not last:
                        t2_ = t - 2
                        make_uT(b, i + 1, t2_, tmpA if t2_ % 2 == 0 else tmpA2, "tmpA%d" % (t2_ % 2), comb=True)
                if t + 1 < NOT:
                    load_hin(t + 1)
                    OMM(t + 1)
                if t < NOT:
                    LN1(t)
                if 0 <= t - 1 < NOT:
                    LN2a(t - 1)
            p.barrier()
            E("dve", lambda e: e.memset(V[:], 2.0), w=["V%d" % t for t in range(NT)])
            E("pool", lambda e: e.memset(QT[64:128, :, :], 0.0), w=["QT%d" % t for t in range(NT)])
            E("pool", lambda e: e.memset(KT[64:128, :, :], 0.0), w=["KT%d" % t for t in range(NT)])
    p.barrier()
    nc._pe_tags = p.pe_tags
    return nc


def _rope_tables():
    t = np.arange(S)
    row = (t // 64).astype(np.float32)
    col = (t % 64).astype(np.float32)
    inv = (10000.0 ** (-np.arange(16, dtype=np.float32) / 16)).astype(np.float32)
    ar = row[:, None] * inv[None]
    ac = col[:, None] * inv[None]
    cx = np.concatenate([np.cos(ar), np.cos(ar), np.cos(ac), np.cos(ac)], axis=1)
    sx = np.concatenate([-np.sin(ar), np.sin(ar), -np.sin(ac), np.sin(ac)], axis=1)
    cx = cx.reshape(16, 128, 64).transpose(1, 0, 2).reshape(128, 16 * 64)
    sx = sx.reshape(16, 128, 64).transpose(1, 0, 2).reshape(128, 16 * 64)
    return np.ascontiguousarray(cx, np.float32), np.ascontiguousarray(sx, np.float32)


_B_CLASSES = [(0, [0, 1, 2, 3]), (1, [0, 1, 2, 3]), (5, [3, 4, 5, 6, 7]), (14, [12, 13, 14, 15]), (15, [12, 13, 14, 15])]


def _b_tables():
    ridx = np.zeros((128, 21, 128), np.int64)
    cidx = np.zeros((128, 21, 128), np.int64)
    allowed = np.zeros((128, 21, 128), bool)
    key = np.arange(128)
    rk, ck = key // 64, key % 64
    rq, cq = key // 64, key % 64
    t = 0
    for m, kts in _B_CLASSES:
        for kt in kts:
            r = 2 * m + rq[None, :]
            kr = 2 * kt + rk[:, None]
            rs = np.clip(r - 4, 0, 24)
            cs = np.clip(cq[None, :] - 8, 0, 48)
            ok = (kr >= rs) & (kr <= rs + 7) & (ck[:, None] >= cs) & (ck[:, None] <= cs + 15)
            ridx[:, t, :] = np.clip(kr - r + 7, 0, 14)
            cidx[:, t, :] = np.clip(ck[:, None] - cq[None, :] + 15, 0, 30)
            allowed[:, t, :] = ok
            t += 1
    return ridx, cidx, allowed


def _prep_shared(inp):
    f = np.float32
    sh = {}
    sh["w_ada"] = np.ascontiguousarray(inp["w_ada"], f)
    b_ada = np.asarray(inp["b_ada"], f)
    sh["badaT"] = np.ascontiguousarray(b_ada[:, :2048].reshape(DEPTH, 16, 128).transpose(2, 0, 1))
    sh["bgate"] = np.ascontiguousarray(b_ada[:, 2048:])
    sh["ln_g"] = np.ascontiguousarray(inp["ln_g"], f)
    sh["ln_b"] = np.ascontiguousarray(inp["ln_b"], f)
    lnT = np.stack([np.asarray(inp["ln_g"], f), np.asarray(inp["ln_b"], f)], axis=0)
    sh["lnT"] = np.ascontiguousarray(lnT.reshape(2, DEPTH, 8, 128).transpose(3, 0, 1, 2)).reshape(128, 2 * DEPTH * 8)
    aw = np.asarray(inp["a_w_in"], f)
    wA = np.zeros((2, 4, 128, 8, 640), f)
    for j in range(4):
        cols = np.concatenate([np.arange(256 * j, 256 * j + 256), 1024 + 64 * j + np.arange(64), 1280 + 64 * j + np.arange(64),
                               1536 + 256 * j + np.arange(256)])
        wA[:, j] = aw[:, :, cols].reshape(2, 8, 128, 640).transpose(0, 2, 1, 3)
    sh["wA"] = wA.reshape(2, 4, 128, 8 * 640)
    bw = np.asarray(inp["b_w_in"], f)
    wB = np.zeros((2, 8, 128, 8, 512), f)
    for pp in range(8):
        cols = np.concatenate([k * 1024 + 128 * pp + np.arange(128) for k in range(4)])
        wB[:, pp] = bw[:, :, cols].reshape(2, 8, 128, 512).transpose(0, 2, 1, 3)
    sh["wB"] = wB.reshape(2, 8, 128, 8 * 512)
    sh["woA"] = np.ascontiguousarray(np.asarray(inp["a_w_out"], f).reshape(2, 8, 128, D).transpose(0, 2, 1, 3)).reshape(2, 128, 8 * D)
    sh["woB"] = np.ascontiguousarray(np.asarray(inp["b_w_out"], f).reshape(2, 8, 128, D).transpose(0, 2, 1, 3)).reshape(2, 128, 8 * D)
    sh["sinkA"] = np.ascontiguousarray(inp["a_sink"], f)
    ridx, cidx, allowed = _b_tables()
    rb = np.asarray(inp["b_rel_bias"], f)
    sh["biasG"] = np.ascontiguousarray(rb[:, :, ridx, cidx]).reshape(2, 16, 128, 21 * 128)
    sh["maskB"] = np.where(allowed, 0.0, -30000.0).astype(f).reshape(128, 21 * 128)
    kp = np.arange(128)[:, None]
    qf = np.arange(128)[None, :]
    sh["maskA"] = np.concatenate([(kp >= qf), (kp <= qf)], axis=1).astype(f)
    sh["cosx"], sh["sinx"] = _rope_tables()
    sh["ident"] = np.eye(128, dtype=f)
    return sh


_NC_CACHE = {}


def kernel(x, c, ctx, c_ctx, w_ada, b_ada, ln_g, ln_b, a_w_in, a_w_out, a_sink, b_w_in, b_w_out, b_rel_bias):
    inp = dict(w_ada=w_ada, b_ada=b_ada, ln_g=ln_g, ln_b=ln_b, a_w_in=a_w_in, a_w_out=a_w_out, a_sink=a_sink,
               b_w_in=b_w_in, b_w_out=b_w_out, b_rel_bias=b_rel_bias)
    sh = _prep_shared(inp)
    x = np.asarray(x, np.float32)
    ctx = np.asarray(ctx, np.float32)
    c = np.asarray(c, np.float32)
    c_ctx = np.asarray(c_ctx, np.float32)
    NB = x.shape[0] // NCORES
    if NB not in _NC_CACHE:
        _NC_CACHE[NB] = build_nc(NB, DEPTH)
    nc = _NC_CACHE[NB]
    in_maps = []
    for core in range(NCORES):
        sl = slice(core * NB, (core + 1) * NB)
        cc = np.concatenate([c[sl], np.broadcast_to(c_ctx[None], (5 - NB, D))], axis=0)[:5] if NB < 5 else None
        cc5 = np.zeros((5, D), np.float32)
        cc5[:NB] = c[sl]
        cc5[4] = c_ctx
        m = dict(sh)
        m["x"] = np.ascontiguousarray(x[sl])
        m["ctx"] = np.ascontiguousarray(ctx[sl])
        m["cT"] = np.ascontiguousarray(cc5.reshape(5, 8, 128).transpose(2, 1, 0))
        in_maps.append(m)
    res = run_bass_kernel_spmd(nc, in_maps, core_ids=list(range(NCORES)))
    return np.concatenate([r["out"] for r in res.results], axis=0)
```

```python
import numpy as np
import concourse.bass as bass
import concourse.mybir as mybir
from concourse.bass_utils import run_bass_kernel_spmd

F32 = mybir.dt.float32
BF16 = mybir.dt.bfloat16
AF = mybir.ActivationFunctionType
ALU = mybir.AluOpType

D = 1024
S = 2048
C = 256
NT = 18
DEPTH = 4
ALPHA = (2.0 * DEPTH) ** 0.25
EPS = 1e-5
NCORES = 8
WARM = 0


class P:
    def __init__(self, nc):
        self.nc = nc
        self.eng = {"pe": nc.tensor, "act": nc.scalar, "dve": nc.vector, "pool": nc.gpsimd, "sp": nc.sync}
        self.sems = {}
        self.cnt = {}
        for k in ["pe", "act", "dve", "pool"]:
            self.sems[k] = nc.alloc_semaphore("e_" + k)
            self.cnt[k] = 0
        self.waited = {k: {} for k in self.eng}
        self.lastw = {}
        self.readers = {}
        self.tag = ""
        self.pe_tags = []

    def _need(self, e, toks):
        for (s, v) in toks:
            if v <= 0:
                continue
            if s == "pe" and e == "pe":
                continue
            if self.waited[e].get(s, 0) >= v:
                continue
            self.eng[e].wait_ge(self.sems[s], v)
            self.waited[e][s] = v

    def emit(self, e, fn, r=(), w=(), dma=None):
        toks = []
        for res in r:
            if res in self.lastw:
                toks.append(self.lastw[res])
        for res in w:
            if res in self.lastw:
                toks.append(self.lastw[res])
            for s, v in self.readers.get(res, {}).items():
                toks.append((s, v))
        self._need(e, toks)
        ins = fn(self.eng[e])
        if e == "pe":
            self.pe_tags.append(self.tag)
        if dma is None:
            self.cnt[e] += 1
            ins.then_inc(self.sems[e], 1)
            tok = (e, self.cnt[e])
        else:
            if dma not in self.sems:
                self.sems[dma] = self.nc.alloc_semaphore("d_" + dma)
                self.cnt[dma] = 0
            self.cnt[dma] += 16
            ins.then_inc(self.sems[dma], 16)
            tok = (dma, self.cnt[dma])
        for res in w:
            self.lastw[res] = tok
            self.readers[res] = {}
        for res in r:
            d = self.readers.setdefault(res, {})
            d[tok[0]] = max(d.get(tok[0], 0), tok[1])
        return tok

    def barrier(self):
        toks = [(s, v) for s, v in self.cnt.items()]
        for e in self.eng:
            self._need(e, toks)


def _layer_cfg(i):
    if i % 2 == 0:
        return dict(kind="A", NG=4, nq=4, nkv=1, XW=320, YW=320, GW=640)
    return dict(kind="B", NG=8, nq=2, nkv=2, XW=256, YW=256, GW=512)


def build_nc(NB=4, NL=DEPTH):
    nc = bass.Bass("TRN2", target_bir_lowering=False)
    dt = nc.dram_tensor
    x = dt("x", [NB, S, D], F32, kind="ExternalInput").ap()
    ctx = dt("ctx", [NB, C, D], F32, kind="ExternalInput").ap()
    out = dt("out", [NB, S, D], F32, kind="ExternalOutput").ap()
    cT = dt("cT", [128, 8, 5], F32, kind="ExternalInput").ap()
    w_ada = dt("w_ada", [DEPTH, D, 3 * D], F32, kind="ExternalInput").ap()
    badaT = dt("badaT", [128, DEPTH, 16], F32, kind="ExternalInput").ap()
    bgate = dt("bgate", [DEPTH, D], F32, kind="ExternalInput").ap()
    ln_g = dt("ln_g", [DEPTH, D], F32, kind="ExternalInput").ap()
    ln_b = dt("ln_b", [DEPTH, D], F32, kind="ExternalInput").ap()
    lnT_d = dt("lnT", [128, 2 * DEPTH * 8], F32, kind="ExternalInput").ap()
    wA = dt("wA", [2, 4, 128, 8 * 640], F32, kind="ExternalInput").ap()
    wB = dt("wB", [2, 8, 128, 8 * 512], F32, kind="ExternalInput").ap()
    woA = dt("woA", [2, 128, 8 * D], F32, kind="ExternalInput").ap()
    woB = dt("woB", [2, 128, 8 * D], F32, kind="ExternalInput").ap()
    sinkA = dt("sinkA", [2, 16], F32, kind="ExternalInput").ap()
    biasG = dt("biasG", [2, 16, 128, 21 * 128], F32, kind="ExternalInput").ap()
    maskB_d = dt("maskB", [128, 21 * 128], F32, kind="ExternalInput").ap()
    maskA_d = dt("maskA", [128, 256], F32, kind="ExternalInput").ap()
    cosx_d = dt("cosx", [128, 16 * 64], F32, kind="ExternalInput").ap()
    sinx_d = dt("sinx", [128, 16 * 64], F32, kind="ExternalInput").ap()
    ident_d = dt("ident", [128, 128], F32, kind="ExternalInput").ap()
    hscr = dt("hscr", [2, S + C, D], F32).ap()
    gscr = dt("gscr", [DEPTH, 5, D], F32).ap()

    sb = nc.alloc_sbuf_tensor
    uT = sb("uT", [128, 8, NT * 128], BF16)
    OGT = sb("OGT", [128, 8, NT * 128], BF16)
    wg = [sb("wg%d" % i, [128, 8, 640], BF16) for i in range(2)]
    wo = sb("wo", [128, 8, D], BF16)
    qk_tm = [sb("qktm%d" % i, [128, 320], BF16) for i in range(2)]
    Pb = [sb("Pb%d" % i, [128, 896], BF16) for i in range(2)]
    EB = sb("EB", [128, 2, 21 * 128], BF16)
    ebraw = sb("ebraw", [128, 11 * 128], F32)
    maskB = sb("maskBs", [128, 21 * 128], BF16)
    maskA = sb("maskAs", [128, 256], BF16)
    cosx = sb("cosxs", [128, 16, 64], F32)
    sinx = sb("sinxs", [128, 16, 64], F32)
    idf = sb("idf", [128, 128], F32)
    idb = sb("idb", [128, 128], BF16)
    modT = sb("modT", [128, DEPTH, 16, 5], F32)
    bada = sb("badas", [128, DEPTH, 16], F32)
    cTs = sb("cTs", [128, 8, 5], F32)
    s2 = sb("s2", [128, 8, 5], F32)
    esink = sb("esink", [128, 16], F32)
    OGb = [sb("OGb%d" % i, [128, 256], BF16) for i in range(2)]
    th2 = [sb("th%d" % i, [128, 256], BF16) for i in range(2)]
    tmpA2 = sb("tmpA2", [128, D], F32)
    tmpB2 = sb("tmpB2", [128, D], F32)
    rt1 = sb("rt1", [128, 320], F32)
    rt2 = sb("rt2", [128, 320], F32)
    sm = [sb("sm%d" % i, [128, 8], F32) for i in range(2)]
    lst = [sb("lst%d" % i, [128, 16], F32) for i in range(2)]
    mhalf = sb("mhalf", [128, 1], F32)
    xin = sb("xin", [128, D], F32)
    lnT = sb("lnTs", [128, 2, DEPTH, 8], F32)
    cmb = sb("cmb", [128, 2, 2, 8], F32)
    onesb = sb("onesb", [1, 128], BF16)
    srow = sb("srow", [1, 2, 16, 65], BF16)
    sinkst = sb("sinkst", [1, 32], F32)
    roff = (nc.sbuf_base + 31) // 32 * 32
    Rbuf = sb("Rbuf", [128, 41600 + 64], mybir.dt.uint8)
    assert nc.sbuf_base >= roff + 41600, (nc.sbuf_base, roff)

    def at(name, shape, dtype, off):
        return nc.alloc_sbuf_tensor_at(name, shape, dtype, offset=roff + off)

    QT = at("QT", [128, 4, NT * 128], BF16, 0)
    KT = at("KT", [128, 2, NT * 128], BF16, 18432)
    V = at("V", [128, NT, 2, 65], BF16, 27648)
    SG = at("SG", [128, NT, 256], BF16, 32384)
    hin = [at("hin%d" % i, [128, D], F32, 4096 * i) for i in range(2)]
    hout = [at("hout%d" % i, [128, D], F32, 8192 + 4096 * i) for i in range(2)]
    tmpA = at("tmpA", [128, D], F32, 16384)
    tmpB = at("tmpB", [128, D], F32, 20480)
    lng = at("lng", [128, D], F32, 24576)
    lnb = at("lnb", [128, D], F32, 28672)
    gate = at("gate", [128, D], F32, 32768)
    gatec = at("gatec", [128, D], F32, 36864)
    wa2 = [at("wa%d" % i_, [128, 8, 512], F32, 16384 * i_) for i_ in range(2)]
    gt2 = [at("gt%d" % i_, [5, 512], F32, 32768 + 2048 * i_) for i_ in range(2)]
    bg5 = at("bg5", [5, D], F32, 36864)

    PSall = nc.alloc_psum_tensor("psall", [128, 8, 512], F32)
    PS = [PSall[:, i, :] for i in range(8)]

    def psb(i, dtype):
        return PS[i][:].bitcast(dtype) if dtype != F32 else PS[i][:]

    p = P(nc)
    E = p.emit

    E("sp", lambda e: e.dma_start(out=idf[:], in_=ident_d[:, :]), w=["idf"], dma="c0")
    E("sp", lambda e: e.dma_start(out=cosx[:].rearrange("p t d -> p (t d)"), in_=cosx_d[:, :]), w=["cosx"], dma="c1")
    E("sp", lambda e: e.dma_start(out=sinx[:].rearrange("p t d -> p (t d)"), in_=sinx_d[:, :]), w=["sinx"], dma="c2")
    E("pool", lambda e: e.dma_start(out=maskB[:], in_=maskB_d[:, :], max_dma_last_dim=4096), w=["maskB"], dma="c3")
    E("pool", lambda e: e.dma_start(out=maskA[:], in_=maskA_d[:, :]), w=["maskA"], dma="c4")
    E("sp", lambda e: e.dma_start(out=cTs[:].rearrange("p k j -> p (k j)"), in_=cT.rearrange("p k j -> p (k j)")), w=["cTs"], dma="c5")
    E("sp", lambda e: e.dma_start(out=bada[:].rearrange("p l c -> p (l c)"), in_=badaT.rearrange("p l c -> p (l c)")), w=["bada"], dma="c6")
    E("dve", lambda e: e.tensor_copy(out=idb[:], in_=idf[:]), r=["idf"], w=["idb"])
    E("sp", lambda e: e.dma_start(out=lnT[:].rearrange("p a l c -> p (a l c)"), in_=lnT_d[:, :]), w=["lnT"], dma="c7")
    E("dve", lambda e: e.memset(mhalf[:], -0.5), w=["mhalf"])
    E("dve", lambda e: e.memset(onesb[:], 1.0), w=["onesb"])
    E("dve", lambda e: e.memset(srow[:], 0.0), w=["srow"])
    E("sp", lambda e: e.dma_start(out=sinkst[:], in_=sinkA.rearrange("l h -> (l h)").partition_broadcast(1)), w=["sinkst"], dma="esink")
    E("act", lambda e: e.activation(out=sinkst[:], in_=sinkst[:], func=AF.Exp), w=["sinkst"])
    for l_ in range(2):
        E("dve", lambda e: e.tensor_scalar(out=srow[0:1, l_, :, 64], in0=sinkst[0:1, l_ * 16:(l_ + 1) * 16], scalar1=2.0, scalar2=None, op0=ALU.mult),
          r=["sinkst"], w=["srow"])
    E("dve", lambda e: e.tensor_scalar(out=bada[:, :, 8:16], in0=bada[:, :, 8:16], scalar1=1.0, scalar2=None, op0=ALU.add), r=["bada"], w=["bada"])
    E("act", lambda e: e.activation(out=s2[:], in_=cTs[:], func=AF.Tanh, scale=0.5), r=["cTs"], w=["s2"])
    E("dve", lambda e: e.scalar_tensor_tensor(out=s2[:], in0=s2[:], scalar=1.0, in1=cTs[:], op0=ALU.add, op1=ALU.mult), r=["s2", "cTs"], w=["s2"])

    blk_i = 0
    for l in range(NL):
        E("sp", lambda e: e.dma_start(out=bg5[:], in_=bgate[l].partition_broadcast(5)), w=["bg5"], dma="bg5")
        for blk in range(6):
            wa = wa2[blk_i % 2]
            war = "wa%d" % (blk_i % 2)
            src = w_ada[l, :, blk * 512:(blk + 1) * 512].rearrange("(k p) n -> p k n", p=128)
            E("sp", lambda e: e.dma_start(out=wa[:], in_=src), w=[war], dma=war)
            if blk < 4:
                for cc in range(4):
                    nch = blk * 4 + cc
                    bk = 4 + cc
                    for k in range(8):
                        E("pe", lambda e: e.matmul(PS[bk][:, 0:5], wa[:, k, cc * 128:(cc + 1) * 128], s2[:, k, :],
                                                   start=(k == 0), stop=(k == 7)), r=[war, "s2"], w=["ps%d" % bk])
                for cc in range(4):
                    nch = blk * 4 + cc
                    bk = 4 + cc
                    E("act", lambda e: e.activation(out=modT[:, l, nch, :], in_=PS[bk][:, 0:5], func=AF.Identity,
                                                    scale=0.5, bias=bada[:, l, nch:nch + 1]), r=["bada"], w=["ps%d" % bk, "modT"])
            else:
                gt = gt2[blk_i % 2]
                gtr = "gt%d" % (blk_i % 2)
                bk = 2 + (blk_i % 2)
                for k in range(8):
                    E("pe", lambda e: e.matmul(PS[bk][0:5, :], s2[:, k, :], wa[:, k, :], start=(k == 0), stop=(k == 7)),
                      r=[war, "s2"], w=["ps%d" % bk])
                c0 = (blk - 4) * 512
                E("act", lambda e: e.activation(out=gt[:], in_=PS[bk][0:5, :], func=AF.Copy, scale=0.5), w=["ps%d" % bk, gtr])
                E("dve", lambda e: e.tensor_tensor(out=gt[:], in0=gt[:], in1=bg5[:, c0:c0 + 512], op=ALU.add), r=["bg5"], w=[gtr])
                E("sp", lambda e: e.dma_start(out=gscr[l, :, c0:c0 + 512], in_=gt[:]), r=[gtr], w=["gscr"], dma=gtr)
            blk_i += 1
    p.barrier()

    def tok_src(b, i, t):
        if i == 0:
            if t < 16:
                return x[b, t * 128:(t + 1) * 128, :]
            return ctx[b, (t - 16) * 128:(t - 15) * 128, :]
        return hscr[(i - 1) % 2, t * 128:(t + 1) * 128, :]

    def hres(i, t):
        return "h%d_%d" % (i, t)

    def make_uT(b, i, t, src_tile, src_res, comb=False):
        j = b if t < 16 else 4
        v_ = 0 if t < 16 else 1
        pb = 4 + 2 * (t % 2)
        for c in range(8):
            bank = pb + c // 4
            E("pe", lambda e: e.transpose(PS[bank][:, (c % 4) * 128:(c % 4 + 1) * 128], src_tile[:, c * 128:(c + 1) * 128], idf[:]),
              r=[src_res, "idf"], w=["ps%d" % bank])
        for c in range(8):
            bank = pb + c // 4
            sc_ = cmb[:, v_, 0, c:c + 1] if comb else modT[:, i, 8 + c, j:j + 1]
            bi_ = cmb[:, v_, 1, c:c + 1] if comb else modT[:, i, c, j:j + 1]
            E("act", lambda e: e.activation(out=uT[:, c, t * 128:(t + 1) * 128], in_=PS[bank][:, (c % 4) * 128:(c % 4 + 1) * 128],
                                            func=AF.Identity, scale=sc_, bias=bi_),
              r=["cmb" if comb else "modT"], w=["ps%d" % bank, "uT%d" % t])

    E("dve", lambda e: e.memset(V[:], 2.0), w=["V%d" % t for t in range(NT)])
    E("dve", lambda e: e.memset(QT[64:128, :, :], 0.0), w=["QT%d" % t for t in range(NT)])
    E("dve", lambda e: e.memset(KT[64:128, :, :], 0.0), w=["KT%d" % t for t in range(NT)])

    for b in range(NB):
        p.barrier()
        p.tag = "init"
        if b == 0:
            for t in range(NT):
                hb = hin[t % 2]
                E("sp", lambda e: e.dma_start(out=hb[:], in_=tok_src(b, 0, t)), w=["hin%d" % (t % 2)], dma="hin%d" % (t % 2))
                make_uT(b, 0, t, hb, "hin%d" % (t % 2))
            p.barrier()
            E("dve", lambda e: e.memset(V[:], 2.0), w=["V%d" % t for t in range(NT)])
            E("dve", lambda e: e.memset(QT[64:128, :, :], 0.0), w=["QT%d" % t for t in range(NT)])
            E("dve", lambda e: e.memset(KT[64:128, :, :], 0.0), w=["KT%d" % t for t in range(NT)])

        for i in range(NL):
            cfg = _layer_cfg(i)
            kind, NG, nq, nkv, XW, YW, GW = (cfg[k] for k in ["kind", "NG", "nq", "nkv", "XW", "YW", "GW"])
            li = i // 2
            ctx_out = i < NL - 1
            last = i == NL - 1
            wsrc = wA if kind == "A" else wB
            wosrc = woA if kind == "A" else woB
            NQT = 18 if ctx_out else 16

            def load_wg(gi, ii=i):
                c_ = _layer_cfg(ii)
                src_ = (wA if c_["kind"] == "A" else wB)[ii // 2, gi]
                dst = wg[gi % 2][:, :, 0:c_["GW"]]
                E("pool", lambda e: e.dma_start(out=dst, in_=src_.rearrange("p (k n) -> p k n", k=8), max_dma_last_dim=4096),
                  w=["wg%d" % (gi % 2)], dma="wg%d" % (gi % 2))

            if b == 0 and i == 0:
                load_wg(0)
            for gi in range(NG):
                if gi + 1 < NG:
                    load_wg(gi + 1)
                    if gi == 0:
                        E("pool", lambda e: e.dma_start(out=wo[:].rearrange("p c n -> p (c n)"), in_=wosrc[li], max_dma_last_dim=4096), w=["wo"], dma="wo")
                elif not (b == NB - 1 and i == NL - 1):
                    load_wg(0, (i + 1) % NL)
                wgt = wg[gi % 2]
                wres = "wg%d" % (gi % 2)
                eb_jobs = []
                if kind == "B":
                    for hh in range(2):
                        for (c0, c1) in [(0, 11 * 128), (11 * 128, 21 * 128)]:
                            eb_jobs.append((hh, c0, c1))

                def EBDMA(j):
                    hh, c0, c1 = eb_jobs[j]
                    E("sp", lambda e: e.dma_start(out=ebraw[:, 0:c1 - c0], in_=biasG[li, gi * 2 + hh, :, c0:c1]), w=["ebraw"], dma="ebraw")

                def EBPROC(j):
                    hh, c0, c1 = eb_jobs[j]
                    E("dve", lambda e: e.tensor_tensor(out=ebraw[:, 0:c1 - c0], in0=ebraw[:, 0:c1 - c0], in1=maskB[:, c0:c1], op=ALU.add),
                      r=["maskB"], w=["ebraw"])
                    E("act", lambda e: e.activation(out=EB[:, hh, c0:c1], in_=ebraw[:, 0:c1 - c0], func=AF.Exp), r=["ebraw"], w=["EB"])

                p.tag = "proj%s" % kind
                nh = XW // 64
                gw = nq * 64

                def MM(t):
                    bx = 2 * (t % 2)
                    by = bx + 1
                    for k in range(8):
                        E("pe", lambda e: e.matmul(PS[bx][:, 0:XW], uT[:, k, t * 128:(t + 1) * 128], wgt[:, k, 0:XW], start=(k == 0), stop=(k == 7)),
                          r=["uT%d" % t, wres], w=["ps%d" % bx])
                    for k in range(8):
                        E("pe", lambda e: e.matmul(PS[by][:, 0:YW], uT[:, k, t * 128:(t + 1) * 128], wgt[:, k, XW:XW + YW], start=(k == 0), stop=(k == 7)),
                          r=["uT%d" % t, wres], w=["ps%d" % by])

                def EV(t):
                    bx = 2 * (t % 2)
                    by = bx + 1
                    qt = qk_tm[t % 2]
                    qres = "qktm%d" % (t % 2)
                    if kind == "A" and t < 16:
                        NH = 5
                        xin = PS[bx][:, 0:320].rearrange("p (h d) -> p h d", h=NH)
                        cb = cosx[:, t, :]
                        cb = bass.AP(cb.tensor, cb.offset, [list(cb.ap[0]), [0, NH], [1, 64]])
                        E("dve", lambda e: e.tensor_tensor(out=rt1[:].rearrange("p (h d) -> p h d", h=NH), in0=xin, in1=cb, op=ALU.mult),
                          r=["cosx"], w=["ps%d" % bx, "rt1"])
                        sbp = sinx[:, t, :]
                        for a_ in range(2):
                            sa = bass.AP(sbp.tensor, sbp.offset + a_ * 16, [list(sbp.ap[0]), [0, NH], [32, 2], [1, 16]])
                            xa = PS[bx][:, 0:320].rearrange("p (h r a f) -> p h r a f", h=NH, r=2, a=2, f=16)[:, :, :, 1 - a_, :]
                            ra = rt2[:].rearrange("p (h r a f) -> p h r a f", h=NH, r=2, a=2, f=16)[:, :, :, a_, :]
                            E("dve", lambda e: e.tensor_tensor(out=ra, in0=xa, in1=sa, op=ALU.mult), r=["sinx"], w=["ps%d" % bx, "rt2"])
                        E("pool", lambda e: e.tensor_tensor(out=qt[:, 0:320], in0=rt1[:], in1=rt2[:], op=ALU.add), r=["rt1", "rt2"], w=[qres])
                    else:
                        E("dve", lambda e: e.tensor_copy(out=qt[:, 0:XW], in_=PS[bx][:, 0:XW]), w=["ps%d" % bx, qres])
                    E("act", lambda e: e.activation(out=V[:, t, 0:nkv, 0:64], in_=PS[by][:, 0:nkv * 64].rearrange("p (h d) -> p h d", h=nkv), func=AF.Copy),
                      w=["ps%d" % by, "V%d" % t])
                    th = th2[t % 2]
                    E("act", lambda e: e.activation(out=th[:, 0:gw], in_=PS[by][:, nkv * 64:nkv * 64 + gw], func=AF.Tanh, scale=0.5),
                      w=["ps%d" % by, "th%d" % (t % 2)])
                    E("dve", lambda e: e.scalar_tensor_tensor(out=SG[:, t, 0:gw], in0=th[:, 0:gw], scalar=1.0, in1=PS[by][:, nkv * 64:nkv * 64 + gw],
                                                              op0=ALU.add, op1=ALU.mult), r=["th%d" % (t % 2)], w=["ps%d" % by, "SG%d" % t])

                def TR(t):
                    qt = qk_tm[t % 2]
                    qres = "qktm%d" % (t % 2)
                    tb = 4 + (t % 2)
                    ptr = PS[tb][:].bitcast(BF16)
                    for hh in range(nh):
                        E("pe", lambda e: e.transpose(ptr[0:64, hh * 128:(hh + 1) * 128], qt[:, hh * 64:(hh + 1) * 64], idb[:]),
                          r=[qres, "idb"], w=["ps%d" % tb])

                def TRE(t):
                    tb = 4 + (t % 2)
                    ptr = PS[tb][:].bitcast(BF16)
                    E("act", lambda e: e.activation(out=QT[0:64, 0:nq, t * 128:(t + 1) * 128],
                                                    in_=ptr[0:64, 0:nq * 128].rearrange("p (h n) -> p h n", h=nq), func=AF.Copy),
                      w=["ps%d" % tb, "QT%d" % t])
                    E("act", lambda e: e.activation(out=KT[0:64, 0:nkv, t * 128:(t + 1) * 128],
                                                    in_=ptr[0:64, nq * 128:(nq + nkv) * 128].rearrange("p (h n) -> p h n", h=nkv), func=AF.Copy),
                      w=["ps%d" % tb, "KT%d" % t])

                if eb_jobs:
                    EBDMA(0)
                MM(0)
                for t in range(NT):
                    if t + 1 < NT:
                        MM(t + 1)
                    EV(t)
                    if t > 0:
                        TRE(t - 1)
                    TR(t)
                    if eb_jobs and t in (3, 7, 11, 15):
                        j = (t - 3) // 4
                        EBPROC(j)
                        if j + 1 < len(eb_jobs):
                            EBDMA(j + 1)
                TRE(NT - 1)

                p.tag = "attn%s" % kind
                its = []
                for m in range(NQT):
                    eb0 = None
                    if m >= 16:
                        kts = [16, 17]
                        nmask = 0
                        mwhich = None
                    elif kind == "A":
                        kts = []
                        mwhich = []
                        if m > 0:
                            kts.append(m - 1)
                            mwhich.append(0)
                        if m < 15:
                            kts.append(m + 1)
                            mwhich.append(1)
                        nmask = len(kts)
                        kts += [m, 16, 17]
                    else:
                        mwhich = None
                        if m == 0:
                            lat, eb0 = [0, 1, 2, 3], 0
                        elif m == 1:
                            lat, eb0 = [0, 1, 2, 3], 4
                        elif m == 14:
                            lat, eb0 = [12, 13, 14, 15], 13
                        elif m == 15:
                            lat, eb0 = [12, 13, 14, 15], 17
                        else:
                            lat, eb0 = [m - 2, m - 1, m, m + 1, m + 2], 8
                        nmask = len(lat)
                        kts = lat + [16, 17]
                    for hq in range(nq):
                        its.append(dict(m=m, hq=hq, kts=kts, nmask=nmask, eb0=eb0, mwhich=mwhich))

                def SA(k):
                    d_ = its[k]
                    m, hq, kts = d_["m"], d_["hq"], d_["kts"]
                    hk = 0 if kind == "A" else hq
                    sbk = 2 * (k % 2)
                    for idx, kt in enumerate(kts):
                        bank = sbk + idx // 4
                        E("pe", lambda e: e.matmul(PS[bank][:, (idx % 4) * 128:(idx % 4 + 1) * 128], KT[:, hk, kt * 128:(kt + 1) * 128],
                                                   QT[:, hq, m * 128:(m + 1) * 128], start=True, stop=True),
                          r=["KT%d" % kt, "QT%d" % m], w=["ps%d" % bank])

                def SB(k):
                    d_ = its[k]
                    hq, kts, nmask, eb0, mwhich = d_["hq"], d_["kts"], d_["nmask"], d_["eb0"], d_["mwhich"]
                    n = len(kts)
                    sbk = 2 * (k % 2)
                    pb_ = Pb[k % 2]
                    pres = "Pb%d" % (k % 2)
                    sview = PSall[:, sbk:sbk + 2, :].rearrange("p b n -> p (b n)")
                    E("act", lambda e: e.activation(out=pb_[:, 0:n * 128], in_=sview[:, 0:n * 128], func=AF.Exp, scale=0.125),
                      w=["ps%d" % sbk, pres] + (["ps%d" % (sbk + 1)] if n > 4 else []))
                    if nmask > 0:
                        if kind == "A":
                            msk = maskA[:, 0:256] if nmask == 2 else maskA[:, mwhich[0] * 128:(mwhich[0] + 1) * 128]
                            E("dve", lambda e: e.tensor_tensor(out=pb_[:, 0:nmask * 128], in0=pb_[:, 0:nmask * 128], in1=msk, op=ALU.mult),
                              r=["maskA"], w=[pres])
                        else:
                            E("dve", lambda e: e.tensor_tensor(out=pb_[:, 0:nmask * 128], in0=pb_[:, 0:nmask * 128],
                                                               in1=EB[:, hq, eb0 * 128:(eb0 + nmask) * 128], op=ALU.mult),
                              r=["EB"], w=[pres])

                def SC(k):
                    d_ = its[k]
                    hq, kts = d_["hq"], d_["kts"]
                    hk = 0 if kind == "A" else hq
                    hglob = gi * nq + hq
                    n = len(kts)
                    obk = 4 + (k % 2)
                    pb_ = Pb[k % 2]
                    pres = "Pb%d" % (k % 2)
                    if kind == "A":
                        E("pe", lambda e: e.matmul(PS[obk][:, 0:65], onesb[0:1, :], srow[0:1, li, hglob, :], start=True, stop=False),
                          r=["srow", "onesb"], w=["ps%d" % obk])
                    for idx, kt in enumerate(kts):
                        E("pe", lambda e: e.matmul(PS[obk][:, 0:65], pb_[:, idx * 128:(idx + 1) * 128], V[:, kt, hk, :],
                                                   start=(idx == 0 and kind != "A"), stop=(idx == n - 1)),
                          r=[pres, "V%d" % kt], w=["ps%d" % obk])

                def SD(k):
                    d_ = its[k]
                    m, hq = d_["m"], d_["hq"]
                    obk = 4 + (k % 2)
                    smt = sm[k % 2]
                    smres = "sm%d" % (k % 2)
                    ogb = OGb[m % 2]
                    ogres = "OGb%d" % (m % 2)
                    E("dve", lambda e: e.reciprocal(out=smt[:, 1:2], in_=PS[obk][:, 64:65]), w=["ps%d" % obk, smres])
                    E("dve", lambda e: e.scalar_tensor_tensor(out=ogb[:, hq * 64:(hq + 1) * 64], in0=PS[obk][:, 0:64], scalar=smt[:, 1:2],
                                                              in1=SG[:, m, hq * 64:(hq + 1) * 64], op0=ALU.mult, op1=ALU.mult),
                      r=[smres, "SG%d" % m], w=["ps%d" % obk, ogres])

                def OGTR_pe(m):
                    ogb = OGb[m % 2]
                    ogres = "OGb%d" % (m % 2)
                    nch = nq // 2
                    pto = PS[6][:].bitcast(BF16)
                    for c_ in range(nch):
                        E("pe", lambda e: e.transpose(pto[:, c_ * 128:(c_ + 1) * 128], ogb[:, c_ * 128:(c_ + 1) * 128], idb[:]),
                          r=[ogres, "idb"], w=["ps6"])

                def OGTR_act(m):
                    nch = nq // 2
                    pto = PS[6][:].bitcast(BF16)
                    ch0 = gi * nch
                    E("act", lambda e: e.activation(out=OGT[:, ch0:ch0 + nch, m * 128:(m + 1) * 128],
                                                    in_=pto[:, 0:nch * 128].rearrange("p (c n) -> p c n", c=nch), func=AF.Copy),
                      w=["ps6", "OGT%d" % m])

                pend = None
                pend2 = None
                SA(0)
                for k in range(len(its) + 1):
                    if k + 1 < len(its):
                        SA(k + 1)
                    if WARM and k < len(its):
                        for _ in range(WARM):
                            E("pe", lambda e: e.matmul(PS[7][:, :], idb[:], uT[:, 0, 0:512], start=True, stop=True),
                              r=["idb", "uT0", "uT1", "uT2", "uT3"], w=["ps7"])
                    if k < len(its):
                        SB(k)
                    if pend2 is not None:
                        OGTR_act(pend2)
                        pend2 = None
                    if k < len(its):
                        SC(k)
                    if pend is not None:
                        OGTR_pe(pend)
                        pend2 = pend
                        pend = None
                    if k > 0:
                        SD(k - 1)
                        if its[k - 1]["hq"] == nq - 1:
                            pend = its[k - 1]["m"]
                if pend is not None:
                    OGTR_pe(pend)
                    pend2 = pend
                if pend2 is not None:
                    OGTR_act(pend2)

            p.barrier()
            p.tag = "outp%s" % kind
            E("sp", lambda e: e.dma_start(out=lng[:], in_=ln_g[i].partition_broadcast(128)), w=["lng"], dma="lng")
            E("sp", lambda e: e.dma_start(out=lnb[:], in_=ln_b[i].partition_broadcast(128)), w=["lnb"], dma="lnb")
            E("sp", lambda e: e.dma_start(out=gate[:], in_=gscr[i, b].partition_broadcast(128)), r=["gscr"], w=["gate"], dma="gate")
            E("sp", lambda e: e.dma_start(out=gatec[:], in_=gscr[i, 4].partition_broadcast(128)), r=["gscr"], w=["gatec"], dma="gatec")
            NOT = 18 if ctx_out else 16
            if not last:
                for v_, j_ in ((0, b), (1, 4)):
                    E("dve", lambda e: e.tensor_tensor(out=cmb[:, v_, 0, :], in0=lnT[:, 0, i, :], in1=modT[:, i + 1, 8:16, j_], op=ALU.mult),
                      r=["lnT", "modT"], w=["cmb"])
                    E("dve", lambda e: e.tensor_tensor(out=cmb[:, v_, 1, :], in0=lnT[:, 1, i, :], in1=modT[:, i + 1, 8:16, j_], op=ALU.mult),
                      r=["lnT", "modT"], w=["cmb"])
                    E("dve", lambda e: e.tensor_tensor(out=cmb[:, v_, 1, :], in0=cmb[:, v_, 1, :], in1=modT[:, i + 1, 0:8, j_], op=ALU.add),
                      r=["modT"], w=["cmb"])

            def load_hin(t):
                E("sp", lambda e: e.dma_start(out=hin[t % 2][:], in_=tok_src(b, i, t)), r=[hres(i - 1, t)] if i > 0 else [],
                  w=["hin%d" % (t % 2)], dma="hin%d" % (t % 2))

            def OMM(t):
                yb = 2 * (t % 2)
                for half in range(2):
                    for c in range(8):
                        E("pe", lambda e: e.matmul(PS[yb + half][:, :], OGT[:, c, t * 128:(t + 1) * 128], wo[:, c, half * 512:(half + 1) * 512],
                                                   start=(c == 0), stop=(c == 7)), r=["OGT%d" % t, "wo"], w=["ps%d" % (yb + half)])

            def bufs(t):
                return dict(yb=2 * (t % 2), gt_=gate if t < 16 else gatec, gres="gate" if t < 16 else "gatec", hi=hin[t % 2], ho=hout[t % 2],
                            ls=lst[t % 2], lres="lst%d" % (t % 2), tA=tmpA if t % 2 == 0 else tmpA2, tB=tmpB if t % 2 == 0 else tmpB2,
                            rA="tmpA%d" % (t % 2), rB="tmpB%d" % (t % 2), hor="hout%d" % (t % 2), hir="hin%d" % (t % 2))

            def LN1(t):
                B_ = bufs(t)
                yb, gt_, gres, hi, ls, lres, tA, tB, rA, rB = (B_[k_] for k_ in ["yb", "gt_", "gres", "hi", "ls", "lres", "tA", "tB", "rA", "rB"])
                yview = PSall[:, yb:yb + 2, :].rearrange("p b n -> p (b n)")
                E("dve", lambda e: e.tensor_tensor(out=tA[:], in0=yview, in1=gt_[:], op=ALU.mult),
                  r=[gres], w=["ps%d" % yb, "ps%d" % (yb + 1), rA])
                E("dve", lambda e: e.scalar_tensor_tensor(out=tB[:], in0=hi[:], scalar=ALPHA, in1=tA[:], op0=ALU.mult, op1=ALU.add),
                  r=[B_["hir"], rA], w=[rB])
                E("dve", lambda e: e.bn_stats(out=ls[:, 0:6], in_=tB[:, 0:512]), r=[rB], w=[lres])
                E("dve", lambda e: e.bn_stats(out=ls[:, 6:12], in_=tB[:, 512:1024]), r=[rB], w=[lres])
                E("dve", lambda e: e.bn_aggr(out=ls[:, 12:14], in_=ls[:, 0:12]), w=[lres])
                E("dve", lambda e: e.tensor_scalar(out=ls[:, 14:15], in0=ls[:, 13:14], scalar1=EPS, scalar2=None, op0=ALU.add), w=[lres])
                E("pool", lambda e: e.tensor_tensor(out=ls[:, 15:16], in0=ls[:, 14:15], in1=mhalf[:], op=ALU.pow), r=["mhalf"], w=[lres])

            def LN2a(t):
                B_ = bufs(t)
                ls, lres, tA, tB, rA, rB = (B_[k_] for k_ in ["ls", "lres", "tA", "tB", "rA", "rB"])
                E("dve", lambda e: e.scalar_tensor_tensor(out=ls[:, 14:15], in0=ls[:, 12:13], scalar=-1.0, in1=ls[:, 15:16], op0=ALU.mult, op1=ALU.mult),
                  w=[lres])
                E("act", lambda e: e.activation(out=tA[:], in_=tB[:], func=AF.Identity, scale=ls[:, 15:16], bias=ls[:, 14:15]),
                  r=[lres, rB], w=[rA])

            def LN2b(t):
                B_ = bufs(t)
                ho, tA, tB, rA, rB, hor = (B_[k_] for k_ in ["ho", "tA", "tB", "rA", "rB", "hor"])
                E("dve", lambda e: e.tensor_tensor(out=tB[:], in0=tA[:], in1=lng[:], op=ALU.mult), r=[rA, "lng"], w=[rB])
                E("pool", lambda e: e.tensor_tensor(out=ho[:], in0=tB[:], in1=lnb[:], op=ALU.add), r=[rB, "lnb"], w=[hor])
                if last:
                    dst = out[b, t * 128:(t + 1) * 128, :]
                    E("sp", lambda e: e.dma_start(out=dst, in_=ho[:]), r=[hor], w=["out"], dma=hor)
                else:
                    dst = hscr[i % 2, t * 128:(t + 1) * 128, :]
                    E("sp", lambda e: e.dma_start(out=dst, in_=ho[:]), r=[hor], w=[hres(i, t)], dma=hor)

            load_hin(0)
            OMM(0)
            for t in range(NOT + 2):
                if 0 <= t - 2 < NOT:
                    LN2b(t - 2)
                    if not last:
                        t2_ = t - 2
                        make_uT(b, i + 1, t2_, tmpA if t2_ % 2 == 0 else tmpA2, "tmpA%d" % (t2_ % 2), comb=True)
                if t + 1 < NOT:
                    load_hin(t + 1)
                    OMM(t + 1)
                if t < NOT:
                    LN1(t)
                if 0 <= t - 1 < NOT:
                    LN2a(t - 1)
                if last and b + 1 < NB and t < NT:
                    E("sp", lambda e: e.dma_start(out=xin[:], in_=tok_src(b + 1, 0, t)), w=["xin"], dma="xin")
                    make_uT(b + 1, 0, t, xin, "xin")
            p.barrier()
            E("dve", lambda e: e.memset(V[:], 2.0), w=["V%d" % t for t in range(NT)])
            E("dve", lambda e: e.memset(QT[64:128, :, :], 0.0), w=["QT%d" % t for t in range(NT)])
            E("dve", lambda e: e.memset(KT[64:128, :, :], 0.0), w=["KT%d" % t for t in range(NT)])
    p.barrier()
    nc._pe_tags = p.pe_tags
    return nc


def _rope_tables():
    t = np.arange(S)
    row = (t // 64).astype(np.float32)
    col = (t % 64).astype(np.float32)
    inv = (10000.0 ** (-np.arange(16, dtype=np.float32) / 16)).astype(np.float32)
    ar = row[:, None] * inv[None]
    ac = col[:, None] * inv[None]
    cx = np.concatenate([np.cos(ar), np.cos(ar), np.cos(ac), np.cos(ac)], axis=1)
    sx = np.concatenate([-np.sin(ar), np.sin(ar), -np.sin(ac), np.sin(ac)], axis=1)
    cx = cx.reshape(16, 128, 64).transpose(1, 0, 2).reshape(128, 16 * 64)
    sx = sx.reshape(16, 128, 64).transpose(1, 0, 2).reshape(128, 16 * 64)
    return np.ascontiguousarray(cx, np.float32), np.ascontiguousarray(sx, np.float32)


_B_CLASSES = [(0, [0, 1, 2, 3]), (1, [0, 1, 2, 3]), (5, [3, 4, 5, 6, 7]), (14, [12, 13, 14, 15]), (15, [12, 13, 14, 15])]


def _b_tables():
    ridx = np.zeros((128, 21, 128), np.int64)
    cidx = np.zeros((128, 21, 128), np.int64)
    allowed = np.zeros((128, 21, 128), bool)
    key = np.arange(128)
    rk, ck = key // 64, key % 64
    rq, cq = key // 64, key % 64
    t = 0
    for m, kts in _B_CLASSES:
        for kt in kts:
            r = 2 * m + rq[None, :]
            kr = 2 * kt + rk[:, None]
            rs = np.clip(r - 4, 0, 24)
            cs = np.clip(cq[None, :] - 8, 0, 48)
            ok = (kr >= rs) & (kr <= rs + 7) & (ck[:, None] >= cs) & (ck[:, None] <= cs + 15)
            ridx[:, t, :] = np.clip(kr - r + 7, 0, 14)
            cidx[:, t, :] = np.clip(ck[:, None] - cq[None, :] + 15, 0, 30)
            allowed[:, t, :] = ok
            t += 1
    return ridx, cidx, allowed


def _prep_shared(inp):
    f = np.float32
    sh = {}
    sh["w_ada"] = np.ascontiguousarray(inp["w_ada"], f)
    b_ada = np.asarray(inp["b_ada"], f)
    sh["badaT"] = np.ascontiguousarray(b_ada[:, :2048].reshape(DEPTH, 16, 128).transpose(2, 0, 1))
    sh["bgate"] = np.ascontiguousarray(b_ada[:, 2048:])
    sh["ln_g"] = np.ascontiguousarray(inp["ln_g"], f)
    sh["ln_b"] = np.ascontiguousarray(inp["ln_b"], f)
    lnT = np.stack([np.asarray(inp["ln_g"], f), np.asarray(inp["ln_b"], f)], axis=0)
    sh["lnT"] = np.ascontiguousarray(lnT.reshape(2, DEPTH, 8, 128).transpose(3, 0, 1, 2)).reshape(128, 2 * DEPTH * 8)
    aw = np.asarray(inp["a_w_in"], f)
    wA = np.zeros((2, 4, 128, 8, 640), f)
    for j in range(4):
        cols = np.concatenate([np.arange(256 * j, 256 * j + 256), 1024 + 64 * j + np.arange(64), 1280 + 64 * j + np.arange(64),
                               1536 + 256 * j + np.arange(256)])
        wA[:, j] = aw[:, :, cols].reshape(2, 8, 128, 640).transpose(0, 2, 1, 3)
    sh["wA"] = wA.reshape(2, 4, 128, 8 * 640)
    bw = np.asarray(inp["b_w_in"], f)
    wB = np.zeros((2, 8, 128, 8, 512), f)
    for pp in range(8):
        cols = np.concatenate([k * 1024 + 128 * pp + np.arange(128) for k in range(4)])
        wB[:, pp] = bw[:, :, cols].reshape(2, 8, 128, 512).transpose(0, 2, 1, 3)
    sh["wB"] = wB.reshape(2, 8, 128, 8 * 512)
    sh["woA"] = np.ascontiguousarray(np.asarray(inp["a_w_out"], f).reshape(2, 8, 128, D).transpose(0, 2, 1, 3)).reshape(2, 128, 8 * D)
    sh["woB"] = np.ascontiguousarray(np.asarray(inp["b_w_out"], f).reshape(2, 8, 128, D).transpose(0, 2, 1, 3)).reshape(2, 128, 8 * D)
    sh["sinkA"] = np.ascontiguousarray(inp["a_sink"], f)
    ridx, cidx, allowed = _b_tables()
    rb = np.asarray(inp["b_rel_bias"], f)
    sh["biasG"] = np.ascontiguousarray(rb[:, :, ridx, cidx]).reshape(2, 16, 128, 21 * 128)
    sh["maskB"] = np.where(allowed, 0.0, -30000.0).astype(f).reshape(128, 21 * 128)
    kp = np.arange(128)[:, None]
    qf = np.arange(128)[None, :]
    sh["maskA"] = np.concatenate([(kp >= qf), (kp <= qf)], axis=1).astype(f)
    sh["cosx"], sh["sinx"] = _rope_tables()
    sh["ident"] = np.eye(128, dtype=f)
    return sh


_NC_CACHE = {}


def kernel(x, c, ctx, c_ctx, w_ada, b_ada, ln_g, ln_b, a_w_in, a_w_out, a_sink, b_w_in, b_w_out, b_rel_bias):
    inp = dict(w_ada=w_ada, b_ada=b_ada, ln_g=ln_g, ln_b=ln_b, a_w_in=a_w_in, a_w_out=a_w_out, a_sink=a_sink,
               b_w_in=b_w_in, b_w_out=b_w_out, b_rel_bias=b_rel_bias)
    sh = _prep_shared(inp)
    x = np.asarray(x, np.float32)
    ctx = np.asarray(ctx, np.float32)
    c = np.asarray(c, np.float32)
    c_ctx = np.asarray(c_ctx, np.float32)
    NB = x.shape[0] // NCORES
    if NB not in _NC_CACHE:
        _NC_CACHE[NB] = build_nc(NB, DEPTH)
    nc = _NC_CACHE[NB]
    in_maps = []
    for core in range(NCORES):
        sl = slice(core * NB, (core + 1) * NB)
        cc = np.concatenate([c[sl], np.broadcast_to(c_ctx[None], (5 - NB, D))], axis=0)[:5] if NB < 5 else None
        cc5 = np.zeros((5, D), np.float32)
        cc5[:NB] = c[sl]
        cc5[4] = c_ctx
        m = dict(sh)
        m["x"] = np.ascontiguousarray(x[sl])
        m["ctx"] = np.ascontiguousarray(ctx[sl])
        m["cT"] = np.ascontiguousarray(cc5.reshape(5, 8, 128).transpose(2, 1, 0))
        in_maps.append(m)
    res = run_bass_kernel_spmd(nc, in_maps, core_ids=list(range(NCORES)))
    return np.concatenate([r["out"] for r in res.results], axis=0)
```
